# Optimizing a Trainium2 kernel written in Bass

```python
import math
import jax
import jax.numpy as jnp
from jax import lax
import numpy as np

D_MODEL = 2048
BATCH = 8
SEQ = 2048
DEPTH = 2

BRANCH_WIDTH = D_MODEL
N_BRANCH = 3
EPS = 1e-6
GMLP_CHUNK = 128
GMLP_GROUPS = 8
GMLP_GROUP_CH = BRANCH_WIDTH // GMLP_GROUPS
GLA_HEADS = 4
GLA_DK = (BRANCH_WIDTH // 2) // GLA_HEADS
GLA_DV = BRANCH_WIDTH // GLA_HEADS
GLA_RANK = 16
GLA_TAU = 16.0
GLA_CHUNK = 64
DIFF_HEADS = 8
DIFF_HEAD_DIM = BRANCH_WIDTH // (2 * DIFF_HEADS)
DIFF_V_DIM = 2 * DIFF_HEAD_DIM
ATTN_BLOCK = 128
REL_BUCKETS = 32
REL_MAX_DIST = 128
IN_SIZES = (BRANCH_WIDTH, BRANCH_WIDTH, BRANCH_WIDTH,
            GLA_HEADS * GLA_DK, GLA_HEADS * GLA_DK, GLA_HEADS * GLA_DV, BRANCH_WIDTH, 2 * GLA_RANK,
            2 * DIFF_HEADS * DIFF_HEAD_DIM, 2 * DIFF_HEADS * DIFF_HEAD_DIM, DIFF_HEADS * DIFF_V_DIM, BRANCH_WIDTH)
D_IN = sum(IN_SIZES)

kernel_name = "hybrid_gmlp_gla_diffattn_encoder"


def rms_norm(x, g):
    xf = x.astype(jnp.float32)
    y = xf * lax.rsqrt(jnp.mean(xf * xf, axis=-1, keepdims=True) + EPS)
    return (y * g.astype(jnp.float32)).astype(x.dtype)


def layer_norm(x, g, b):
    xf = x.astype(jnp.float32)
    mu = jnp.mean(xf, axis=-1, keepdims=True)
    var = jnp.mean(jnp.square(xf - mu), axis=-1, keepdims=True)
    y = (xf - mu) * lax.rsqrt(var + EPS)
    return (y * g.astype(jnp.float32) + b.astype(jnp.float32)).astype(x.dtype)


def t5_buckets(rel):
    nb = REL_BUCKETS // 2
    max_exact = nb // 2
    ret = jnp.where(rel > 0, nb, 0).astype(jnp.int32)
    n = jnp.abs(rel).astype(jnp.int32)
    nf = jnp.maximum(n, 1).astype(jnp.float32)
    large = max_exact + (jnp.log(nf / max_exact) / math.log(REL_MAX_DIST / max_exact)
                         * (nb - max_exact)).astype(jnp.int32)
    large = jnp.minimum(large, nb - 1)
    return ret + jnp.where(n < max_exact, n, large)


def gla_chunked(q, k, v, g):
    out_dtype = v.dtype
    b_, s_, h_, dk = q.shape
    dv = v.shape[-1]
    n_ch = s_ // GLA_CHUNK

    def to_chunks(t):
        return t.astype(jnp.float32).reshape(b_, n_ch, GLA_CHUNK, h_, t.shape[-1]).transpose(1, 0, 3, 2, 4)

    q, k, v, g = (to_chunks(t) for t in (q, k, v, g))
    cum = jnp.cumsum(g, axis=-2)
    ref = cum[..., GLA_CHUNK // 2 - 1:GLA_CHUNK // 2, :]
    last = cum[..., -1:, :]
    scores = jnp.einsum('nbhid,nbhjd->nbhij', q * jnp.exp(cum - ref), k * jnp.exp(ref - cum))
    lower_tri = jnp.tril(jnp.ones((GLA_CHUNK, GLA_CHUNK), dtype=bool))
    scores = jnp.where(lower_tri, scores, 0.0)
    o_intra = jnp.einsum('nbhij,nbhje->nbhie', scores, v)
    q_inter = q * jnp.exp(cum)
    k_state = k * jnp.exp(last - cum)
    chunk_decay = jnp.exp(last[..., 0, :])

    def step(state, xs):
        qn, kn, vn, dn = xs
        o = jnp.einsum('bhid,bhde->bhie', qn, state)
        state = dn[..., None] * state + jnp.einsum('bhid,bhie->bhde', kn, vn)
        return state, o

    state0 = jnp.zeros((b_, h_, dk, dv), jnp.float32)
    _, o_inter = lax.scan(step, state0, (q_inter, k_state, v, chunk_decay))
    o = (o_intra + o_inter).transpose(1, 0, 3, 2, 4).reshape(b_, s_, h_, dv)
    return o.astype(out_dtype)


def diff_attention(q, k, v, rel_bias, lam):
    b_, s_, h_, _, d = q.shape
    nb = s_ // ATTN_BLOCK
    qb = q.reshape(b_, nb, ATTN_BLOCK, h_, 2, d).transpose(1, 0, 2, 3, 4, 5)
    starts = jnp.arange(nb, dtype=jnp.int32) * ATTN_BLOCK
    k_pos = jnp.arange(s_, dtype=jnp.int32)
    scale = d ** -0.5

    def block(args):
        qblk, start = args
        q_pos = start + jnp.arange(ATTN_BLOCK, dtype=jnp.int32)
        bucket = t5_buckets(k_pos[None, :] - q_pos[:, None])
        bias = jnp.transpose(rel_bias[bucket], (2, 0, 1)).astype(jnp.float32)
        s = jnp.einsum('bqhcd,bkhcd->bhcqk', qblk, k).astype(jnp.float32) * scale + bias[None, :, None]
        p = jax.nn.softmax(s, axis=-1)
        a = p[:, :, 0] - lam * p[:, :, 1]
        return jnp.einsum('bhqk,bkhe->bqhe', a.astype(v.dtype), v)

    out = lax.map(block, (qb, starts))
    return out.transpose(1, 0, 2, 3, 4).reshape(b_, s_, h_, v.shape[-1])


def setup_inputs(seed: int = 0) -> dict:
    key = jax.random.key(seed)
    ks = jax.random.split(key, 18)
    f32 = jnp.float32
    L = DEPTH
    nrm = lambda k, shp: jax.random.normal(k, shp, f32)
    return {
        "x": nrm(ks[0], (BATCH, SEQ, D_MODEL)),
        "norm_pre": 1.0 + 0.02 * nrm(ks[1], (L, D_MODEL)),
        "w_in": nrm(ks[2], (L, D_MODEL, D_IN)) * D_MODEL ** -0.5,
        "gmlp_ln_g": 1.0 + 0.02 * nrm(ks[3], (L, BRANCH_WIDTH)),
        "gmlp_ln_b": 0.02 * nrm(ks[4], (L, BRANCH_WIDTH)),
        "gmlp_ws": nrm(ks[5], (L, GMLP_GROUPS, GMLP_CHUNK, GMLP_CHUNK)) * GMLP_CHUNK ** -0.5,
        "gmlp_bs": 1.0 + 0.1 * nrm(ks[6], (L, GMLP_GROUPS, GMLP_CHUNK)),
        "gla_wa2": nrm(ks[7], (L, 2, GLA_RANK, GLA_HEADS * GLA_DK)) * GLA_RANK ** -0.5,
        "gla_ba": 0.1 * nrm(ks[8], (L, 2, GLA_HEADS * GLA_DK)),
        "gla_norm": 1.0 + 0.02 * nrm(ks[9], (L, GLA_DV)),
        "diff_lambda": 0.1 * nrm(ks[10], (L, 4, DIFF_HEAD_DIM)),
        "diff_norm": 1.0 + 0.02 * nrm(ks[11], (L, DIFF_V_DIM)),
        "rel_bias": 0.3 * nrm(ks[12], (REL_BUCKETS, DIFF_HEADS)),
        "w_branch": nrm(ks[13], (L, N_BRANCH, BRANCH_WIDTH, D_MODEL)) * BRANCH_WIDTH ** -0.5,
        "w_merge": nrm(ks[14], (L, D_MODEL, N_BRANCH * D_MODEL)) * D_MODEL ** -0.5,
        "b_merge": 0.1 * nrm(ks[15], (L, N_BRANCH * D_MODEL)),
        "w_out": nrm(ks[16], (L, D_MODEL, D_MODEL)) * D_MODEL ** -0.5,
        "norm_post": 1.0 + 0.02 * nrm(ks[17], (L, D_MODEL)),
    }


def reference(x, norm_pre, w_in, gmlp_ln_g, gmlp_ln_b, gmlp_ws, gmlp_bs, gla_wa2, gla_ba, gla_norm,
              diff_lambda, diff_norm, rel_bias, w_branch, w_merge, b_merge, w_out, norm_post):
    B, S, _ = x.shape
    split_points = []
    acc = 0
    for size in IN_SIZES[:-1]:
        acc += size
        split_points.append(acc)
    rev = lambda t: jnp.flip(t, axis=1)

    for l in range(DEPTH):
        h = rms_norm(x, norm_pre[l])
        proj = jnp.einsum('bsd,de->bse', h, w_in[l])
        (a_u, a_v, a_z, b_q, b_k, b_v, b_z, b_lr,
         c_q, c_k, c_v, c_z) = jnp.split(proj, split_points, axis=-1)

        u = jax.nn.gelu(a_u)
        sv = layer_norm(jax.nn.gelu(a_v), gmlp_ln_g[l], gmlp_ln_b[l])
        sv = sv.reshape(B, S // GMLP_CHUNK, GMLP_CHUNK, GMLP_GROUPS, GMLP_GROUP_CH)
        sv = jnp.einsum('gpq,bnqgc->bnpgc', gmlp_ws[l], sv) + gmlp_bs[l].T[None, None, :, :, None]
        y_a = u * sv.reshape(B, S, BRANCH_WIDTH)

        q = b_q.reshape(B, S, GLA_HEADS, GLA_DK) * GLA_DK ** -0.5
        k = b_k.reshape(B, S, GLA_HEADS, GLA_DK)
        v = b_v.reshape(B, S, GLA_HEADS, GLA_DV)
        lr_f = b_lr[..., :GLA_RANK]
        lr_b = b_lr[..., GLA_RANK:]
        g_f = jax.nn.log_sigmoid((jnp.einsum('bsr,rk->bsk', lr_f, gla_wa2[l, 0]) + gla_ba[l, 0])
                                 .astype(jnp.float32)) / GLA_TAU
        g_b = jax.nn.log_sigmoid((jnp.einsum('bsr,rk->bsk', lr_b, gla_wa2[l, 1]) + gla_ba[l, 1])
                                 .astype(jnp.float32)) / GLA_TAU
        g_f = g_f.reshape(B, S, GLA_HEADS, GLA_DK)
        g_b = g_b.reshape(B, S, GLA_HEADS, GLA_DK)
        o_f = gla_chunked(q, k, v, g_f)
        o_b = rev(gla_chunked(rev(q), rev(k), rev(v), rev(g_b)))
        y_b = rms_norm(o_f + o_b, gla_norm[l]).reshape(B, S, GLA_HEADS * GLA_DV)

        lam_init = 0.8 - 0.6 * math.exp(-0.3 * l)
        lv = diff_lambda[l].astype(jnp.float32)
        lam = jnp.exp(jnp.sum(lv[0] * lv[1])) - jnp.exp(jnp.sum(lv[2] * lv[3])) + lam_init
        qc = c_q.reshape(B, S, DIFF_HEADS, 2, DIFF_HEAD_DIM)
        kc = c_k.reshape(B, S, DIFF_HEADS, 2, DIFF_HEAD_DIM)
        vc = c_v.reshape(B, S, DIFF_HEADS, DIFF_V_DIM)
        o_c = diff_attention(qc, kc, vc, rel_bias, lam)
        y_c = (rms_norm(o_c, diff_norm[l]) * (1.0 - lam_init)).reshape(B, S, DIFF_HEADS * DIFF_V_DIM)

        branches = jnp.stack([y_a * jax.nn.silu(a_z), y_b * jax.nn.silu(b_z), y_c * jax.nn.silu(c_z)], axis=2)
        proj_b = jnp.einsum('bsiw,iwd->bsid', branches, w_branch[l])
        gates = jax.nn.sigmoid(jnp.einsum('bsd,de->bse', h, w_merge[l]) + b_merge[l])
        gates = gates.reshape(B, S, N_BRANCH, D_MODEL)
        merged = jnp.sum(gates * proj_b, axis=2)
        out = jnp.einsum('bsd,de->bse', merged, w_out[l])
        x = x + rms_norm(out, norm_post[l])
    return x
```

```python
import math
import numpy as np
import concourse.bass as bass
import concourse.mybir as mybir
from concourse.bass_utils import run_bass_kernel_spmd

F32 = mybir.dt.float32
BF16 = mybir.dt.bfloat16
AF = mybir.ActivationFunctionType
ALU = mybir.AluOpType
AX = mybir.AxisListType

D = 2048
S = 2048
DEPTH = 2
NT = S // 128
NC_ = D // 128
D_IN = 20512
EPS = 1e-6
N_CORES = 8

OFF = {}
_acc = 0
for _n, _s in (("a_u", 2048), ("a_v", 2048), ("a_z", 2048), ("b_q", 1024), ("b_k", 1024), ("b_v", 2048),
               ("b_z", 2048), ("b_lr", 32), ("c_q", 2048), ("c_k", 2048), ("c_v", 2048), ("c_z", 2048)):
    OFF[_n] = _acc
    _acc += _s
assert _acc == D_IN


class Res:
    __slots__ = ("name", "w", "r", "excl")

    def __init__(self, name, excl=False):
        self.name = name
        self.excl = excl
        self.w = None
        self.r = {}


class FW:
    LIM = 16000
    NDMA = 24

    def __init__(self, nc):
        self.nc = nc
        self.eng = {"pe": nc.tensor, "act": nc.scalar, "dve": nc.vector, "pool": nc.gpsimd, "sp": nc.sync}
        self.sems = {}
        self.cnt = {}
        self.seen = {e: {} for e in self.eng}
        self.pend = {e: ([], []) for e in self.eng}
        self.dma_rr = 0
        self.dma_rr_pool = 0
        self.dma_last = {}
        self.n_inst = {e: 0 for e in self.eng}

    def sem(self, g, ep):
        k = (g, ep)
        if k not in self.sems:
            self.sems[k] = self.nc.alloc_semaphore(name=f"s_{g}_{ep}")
        return self.sems[k]

    def _next(self, g, inc):
        ep, v = self.cnt.get(g, (0, 0))
        if v + inc > self.LIM:
            ep, v = ep + 1, 0
        v += inc
        self.cnt[g] = (ep, v)
        return (g, ep, v)

    def _wait(self, e, toks):
        best = {}
        for t in toks:
            if t is None:
                continue
            g, ep, v = t
            if g not in best or best[g] < (ep, v):
                best[g] = (ep, v)
        for g, (ep, v) in best.items():
            if self.seen[e].get(g, (-1, 0)) < (ep, v):
                self.eng[e].wait_ge(self.sem(g, ep), v)
                self.n_inst[e] += 1
                self.seen[e][g] = (ep, v)

    def _deps(self, e, reads, writes, same_raw=True):
        toks = []
        for r in reads:
            if r.w is not None and (r.w[0] != e or (same_raw and e != "pe")):
                toks.append(r.w)
        strict = (e == "pool")
        for w in writes:
            if w.w is not None and (w.w[0] != e or e != "pe"):
                toks.append(w.w)
            for g, t in w.r.items():
                if g != e or strict:
                    toks.append(t)
        return toks

    def _update(self, tok, reads, writes):
        g = tok[0]
        for r in reads:
            r.r[g] = tok
        for w in writes:
            w.w = tok
            w.r = {}

    def op(self, e, fn, reads=(), writes=(), inc=True):
        if any(r.excl for r in reads):
            writes = list(writes) + [r for r in reads if r.excl]
            reads = [r for r in reads if not r.excl]
        self._wait(e, self._deps(e, reads, writes))
        ins = fn(self.eng[e])
        self.n_inst[e] += 1
        pr, pw = self.pend[e]
        if not inc:
            pr.extend(reads)
            pw.extend(writes)
            return ins
        tok = self._next(e, 1)
        ins.then_inc(self.sem(tok[0], tok[1]), 1)
        self._update(tok, list(reads) + pr, list(writes) + pw)
        self.pend[e] = ([], [])
        return ins

    def dma(self, e, out, in_, reads=(), writes=(), slow=False):
        if e == "pool":
            g = f"q{self.dma_rr_pool % 6}"
            self.dma_rr_pool += 1
        else:
            g = f"d{self.dma_rr % self.NDMA}"
            self.dma_rr += 1
        toks = self._deps(e, reads, writes)
        toks.append(self.dma_last.get(g))
        self._wait(e, toks)
        tok = self._next(g, 16)
        kw = dict(allow_slow_non_contiguous=True) if slow else {}
        self.eng[e].dma_start(out=out, in_=in_, **kw).then_inc(self.sem(tok[0], tok[1]), 16)
        self.n_inst[e] += 1
        self.dma_last[g] = tok
        self._update(tok, reads, writes)
        return tok

    def barrier(self):
        toks = []
        for g, (ep, v) in self.cnt.items():
            if v > 0:
                toks.append((g, ep, v))
        for e in self.eng:
            assert not self.pend[e][0] and not self.pend[e][1], f"pending non-inc ops on {e}"
            self._wait(e, toks)


class Prog:
    def __init__(self, depth=DEPTH, debug=(), stop_after=None):
        self.depth = depth
        self.debug = set(debug)
        self.stop_after = stop_after
        self.skip = None
        nc = bass.Bass("TRN2", target_bir_lowering=False)
        self.nc = nc
        self.fw = FW(nc)
        L = DEPTH
        dt = lambda name, shape, dtype=F32: nc.dram_tensor(name, list(shape), dtype, kind="ExternalInput").ap()
        self.x = dt("x", [S, D])
        self.norm_pre = dt("norm_pre", [L, D])
        self.w_in = dt("w_in", [L, D, D_IN])
        self.gmlp_ln_g = dt("gmlp_ln_g", [L, D])
        self.gmlp_ln_b = dt("gmlp_ln_b", [L, D])
        self.gmlp_ws = dt("gmlp_ws", [L, 8, 128, 128])
        self.gmlp_bs = dt("gmlp_bs", [L, 8, 128])
        self.gla_wa2 = dt("gla_wa2", [L, 2, 16, 1024])
        self.gla_ba = dt("gla_ba", [L, 2, 1024])
        self.gla_norm = dt("gla_norm", [L, 512])
        self.diff_lambda = dt("diff_lambda", [L, 4, 128])
        self.diff_norm = dt("diff_norm", [L, 256])
        self.bias_tiles = dt("bias_tiles", [8, 3, 128, 128])
        self.bias_far = dt("bias_far", [8, 2])
        self.bias_pat = dt("bias_pat", [8, 6, 128, 512])
        self.w_branch = dt("w_branch", [L, 3, D, D])
        self.w_merge = dt("w_merge", [L, D, 3 * D])
        self.b_merge = dt("b_merge", [L, 3 * D])
        self.w_out = dt("w_out", [L, D, D])
        self.norm_post = dt("norm_post", [L, D])
        self.consts = dt("consts", [8, 128, 128])
        self.out = nc.dram_tensor("out", [S, D], F32, kind="ExternalOutput").ap()

        def scratch(name, shape, dtype=BF16):
            kind = "ExternalOutput" if name in self.debug else "Internal"
            return nc.dram_tensor(name, list(shape), dtype, kind=kind).ap()

        self.scratch = scratch
        self.UT = scratch("UT", [D, S])
        self.ZAT = scratch("ZAT", [D, S])
        self.GV = scratch("GV", [S, D])
        self.QBT = scratch("QBT", [1024, S])
        self.KBT = scratch("KBT", [1024, S])
        self.KB = scratch("KB", [S, 1024])
        self.VB = scratch("VB", [S, D])
        self.ZBT = scratch("ZBT", [D, S])
        self.LRT = scratch("LRT", [32, S], F32)
        self.QCT = scratch("QCT", [D, S])
        self.KCT = scratch("KCT", [D, S])
        self.VC = scratch("VC", [S, D])
        self.ZCT = scratch("ZCT", [D, S])
        self.GT = scratch("GT", [3 * D, S])
        self.BRT = scratch("BRT", [3, D, S])
        self.X1 = scratch("X1", [S, D], F32)
        self.OF = scratch("OF", [S, D], F32)
        self.WB = scratch("WB", [3, D, D])
        self.WO = scratch("WO", [D, D])

        self.ARENA_F32 = 51200
        self.arena = nc.alloc_sbuf_tensor("arena", [128, self.ARENA_F32], F32)
        self.psum = [nc.alloc_psum_tensor(f"ps{i}", [128, 512], F32) for i in range(8)]
        self.ps_res = [Res(f"ps{i}", excl=True) for i in range(8)]
        self.ps_rr = 0

    def carve_reset(self):
        self._carve = 0

    def carve(self, nbytes, dtype, shape=None):
        nbytes = (nbytes + 63) // 64 * 64
        a = self._carve // 4
        self._carve += nbytes
        assert self._carve <= self.ARENA_F32 * 4, f"arena overflow {self._carve}"
        ap = self.arena[:, a:a + nbytes // 4]
        if dtype != F32:
            ap = ap.bitcast(dtype)
        return ap

    def bank(self):
        i = self.ps_rr % 8
        self.ps_rr += 1
        return self.psum[i][:, :], self.ps_res[i]

    def phase_norm(self, l, x_src, hT, hT_res):
        fw = self.fw
        ident = self.carve(256, BF16)
        identf = self.carve(512, F32)
        gb = self.carve(8192, F32)
        xt = [self.carve(8192, F32) for _ in range(2)]
        hb = [self.carve(4096, BF16) for _ in range(2)]
        junk = self.carve(4096, BF16)
        st = [self.carve(64, F32) for _ in range(2)]
        r_ident, r_gb, r_junk = Res("ident"), Res("gb"), Res("junk")
        r_xt = [Res("xt0"), Res("xt1")]
        r_hb = [Res("hb0"), Res("hb1")]
        r_st = [Res("st0"), Res("st1")]
        fw.dma("sp", out=identf, in_=self.consts[0], writes=[r_ident])
        fw.op("dve", lambda e: e.tensor_copy(out=ident, in_=identf), reads=[r_ident], writes=[r_ident])
        fw.dma("sp", out=gb, in_=self.norm_pre[l].partition_broadcast(128), writes=[r_gb])
        for t in range(NT):
            s = t % 2
            fw.dma("sp", out=xt[s], in_=x_src[t * 128:(t + 1) * 128, :], writes=[r_xt[s]])
            fw.op("act", lambda e: e.activation(out=junk, in_=xt[s], func=AF.Square, accum_out=st[s][:, 0:1]),
                  reads=[r_xt[s]], writes=[r_junk, r_st[s]])
            fw.op("act", lambda e: e.activation(out=st[s][:, 1:2], in_=st[s][:, 0:1], func=AF.Sqrt, scale=1.0 / D, bias=self.eps_tile),
                  reads=[r_st[s], self.r_const], writes=[r_st[s]])
            fw.op("dve", lambda e: e.reciprocal(out=st[s][:, 2:3], in_=st[s][:, 1:2]), reads=[r_st[s]], writes=[r_st[s]])
            fw.op("dve", lambda e: e.scalar_tensor_tensor(out=hb[s], in0=xt[s], scalar=st[s][:, 2:3], in1=gb,
                                                          op0=ALU.mult, op1=ALU.mult),
                  reads=[r_xt[s], r_st[s], r_gb], writes=[r_hb[s]])
            for half in range(2):
                pt, r_pt = self.bank()
                ptb = pt.bitcast(BF16)
                for c8 in range(8):
                    c = half * 8 + c8
                    fw.op("pe", lambda e: e.transpose(ptb[:, c8 * 128:(c8 + 1) * 128], hb[s][:, c * 128:(c + 1) * 128], ident),
                          reads=[r_hb[s], r_ident], writes=[r_pt], inc=(c8 == 7))
                eng = "act" if half == 0 else "dve"
                src = ptb.rearrange("p (c t) -> p c t", c=8)
                dst = hT[:, half * 8:(half + 1) * 8, t * 128:(t + 1) * 128]
                if eng == "act":
                    fw.op("act", lambda e: e.activation(out=dst, in_=src, func=AF.Copy), reads=[r_pt], writes=[hT_res])
                else:
                    fw.op("dve", lambda e: e.tensor_copy(out=dst, in_=src), reads=[r_pt], writes=[hT_res])

    def proj_blocks(self, actT, act_res, blocks, nwbuf=2, side_dmas=()):
        fw = self.fw
        wbuf = [self.carve(16 * 512 * 2, BF16).rearrange("p (c n) -> p c n", c=16) for _ in range(nwbuf)]
        r_w = [Res(f"w{i}") for i in range(nwbuf)]
        NSTG = 3
        stg = [self.carve(8192, F32) for _ in range(NSTG)]
        r_stg = [Res(f"stg{i}") for i in range(NSTG)]
        si = 0
        side_dmas = list(side_dmas)
        every = max(1, len(blocks) // (len(side_dmas) + 1)) if side_dmas else 0
        for bi, b in enumerate(blocks):
            if side_dmas and bi % every == every - 1:
                o_, i_ = side_dmas.pop(0)
                fw.dma("pool", out=o_, in_=i_)
            ws = bi % nwbuf
            ncols = b["w"].shape[1]
            odt = b.get("dtype", BF16)
            func = b.get("func", AF.Copy)
            scale = b.get("scale", 1.0)
            fw.dma("pool", out=wbuf[ws][:, :, :ncols], in_=b["w"].rearrange("(c p) n -> p c n", p=128), writes=[r_w[ws]])
            if b["orient"] == "F":
                for j in range((ncols + 127) // 128):
                    m = min(128, ncols - j * 128)
                    s_ = si % NSTG
                    si += 1
                    so = stg[s_] if odt == F32 else stg[s_].bitcast(BF16)
                    for tg in range(4):
                        ps, r_ps = self.bank()
                        for c in range(16):
                            fw.op("pe", lambda e: e.matmul(ps[:m, :], wbuf[ws][:, c, j * 128:j * 128 + m], actT[:, c, tg * 512:(tg + 1) * 512],
                                                           start=(c == 0), stop=(c == 15)),
                                  reads=[r_w[ws], act_res], writes=[r_ps], inc=(c == 15))
                        bias = b["bias"][:m, j:j + 1] if b.get("bias") is not None else 0.0
                        rd = [r_ps] + ([self.r_const] if b.get("bias") is not None else [])
                        fw.op("act", lambda e: e.activation(out=so[:m, tg * 512:(tg + 1) * 512], in_=ps[:m, :], func=func, scale=scale, bias=bias),
                              reads=rd, writes=[r_stg[s_]])
                    fw.dma("sp", out=b["dst"][j * 128:j * 128 + m, :], in_=so[:m, 0:2048], reads=[r_stg[s_]])
            else:
                assert ncols == 512
                for t4 in range(4):
                    s_ = si % NSTG
                    si += 1
                    so = stg[s_] if odt == F32 else stg[s_].bitcast(BF16)
                    for tt in range(4):
                        t = t4 * 4 + tt
                        ps, r_ps = self.bank()
                        for c in range(16):
                            fw.op("pe", lambda e: e.matmul(ps[:, :], actT[:, c, t * 128:(t + 1) * 128], wbuf[ws][:, c, :],
                                                           start=(c == 0), stop=(c == 15)),
                                  reads=[r_w[ws], act_res], writes=[r_ps], inc=(c == 15))
                        fw.op("act", lambda e: e.activation(out=so[:, tt * 512:(tt + 1) * 512], in_=ps[:, :], func=func, scale=scale),
                              reads=[r_ps], writes=[r_stg[s_]])
                    dst = b["dst"][t4 * 512:(t4 + 1) * 512, :].rearrange("(tt p) n -> p tt n", p=128)
                    fw.dma("sp", out=dst, in_=so[:, 0:2048].rearrange("p (tt n) -> p tt n", tt=4), reads=[r_stg[s_]])

        for o_, i_ in side_dmas:
            fw.dma("pool", out=o_, in_=i_)

    def inproj_blocks(self, l, bm_tile):
        W = self.w_in[l]
        blocks = []

        def add(name, width, orient, dst, func=AF.Copy, scale=1.0, dtype=BF16):
            o = OFF[name]
            step = 512
            for c0 in range(0, width, step):
                n = min(step, width - c0)
                d = dst[c0:c0 + n, :] if orient == "F" else dst[:, c0:c0 + n]
                blocks.append(dict(w=W[:, o + c0:o + c0 + n], orient=orient, func=func, scale=scale, dst=d, dtype=dtype))

        add("a_u", 2048, "F", self.UT, AF.Gelu_apprx_tanh)
        add("a_v", 2048, "T", self.GV, AF.Gelu_apprx_tanh)
        add("a_z", 2048, "F", self.ZAT, AF.Silu)
        add("b_q", 1024, "F", self.QBT, AF.Copy, 1.0 / 16.0)
        add("b_k", 1024, "F", self.KBT)
        add("b_v", 2048, "T", self.VB)
        add("b_z", 2048, "F", self.ZBT, AF.Silu)
        add("b_lr", 32, "F", self.LRT, AF.Copy, 1.0, F32)
        add("c_q", 2048, "F", self.QCT, AF.Copy, 128.0 ** -0.5)
        add("c_k", 2048, "F", self.KCT)
        add("c_v", 2048, "T", self.VC)
        add("c_z", 2048, "F", self.ZCT, AF.Silu)
        Wm = self.w_merge[l]
        for c0 in range(0, 3 * D, 512):
            blocks.append(dict(w=Wm[:, c0:c0 + 512], orient="F", func=AF.Sigmoid, dst=self.GT[c0:c0 + 512, :],
                               bias=bm_tile[:, c0 // 128:c0 // 128 + 4]))
        return blocks

    def load_ident(self):
        fw = self.fw
        cf = self.carve(8 * 128 * 4, F32).rearrange("p (k n) -> p k n", k=8)
        cb = self.carve(8 * 128 * 2, BF16).rearrange("p (k n) -> p k n", k=8)
        r = Res("cmat")
        fw.dma("sp", out=cf, in_=self.consts.rearrange("k p n -> p k n"), writes=[r])
        fw.op("dve", lambda e: e.tensor_copy(out=cb, in_=cf), reads=[r], writes=[r])
        return cf, cb, r

    def rstd_from(self, ss_ap, out_ap, tmp_ap, n, res):
        fw = self.fw
        fw.op("act", lambda e: e.activation(out=tmp_ap, in_=ss_ap, func=AF.Ln, scale=1.0 / n, bias=self.eps_tile),
              reads=[res, self.r_const], writes=[res])
        fw.op("act", lambda e: e.activation(out=out_ap, in_=tmp_ap, func=AF.Exp, scale=-0.5), reads=[res], writes=[res])

    def phase_mixer_a(self, l):
        fw = self.fw
        cf, cb, r_c = self.load_ident()
        ident = cb[:, 0, :]
        lng = self.carve(8192, F32)
        lnb = self.carve(8192, F32)
        bsf = self.carve(16 * 128 * 4, F32).rearrange("p (c n) -> p c n", c=16)
        wsn = self.carve(8 * 128 * 4, F32).rearrange("p (g q) -> p g q", g=8)
        wsb = self.carve(8 * 128 * 2, BF16).rearrange("p (g q) -> p g q", g=8)
        wsT = self.carve(8 * 128 * 2, BF16).rearrange("p (g q) -> p g q", g=8)
        r_p = Res("a_params")
        fw.dma("sp", out=lng, in_=self.gmlp_ln_g[l].partition_broadcast(128), writes=[r_p])
        fw.dma("sp", out=lnb, in_=self.gmlp_ln_b[l].partition_broadcast(128), writes=[r_p])
        bsv = bsf.rearrange("p (g two) n -> p g two n", two=2)
        for two in range(2):
            fw.dma("sp", out=bsv[:, :, two, :], in_=self.gmlp_bs[l].partition_broadcast(128), writes=[r_p])
        fw.dma("sp", out=wsn, in_=self.gmlp_ws[l].rearrange("g p q -> p g q"), writes=[r_p])
        fw.op("dve", lambda e: e.tensor_copy(out=wsb, in_=wsn), reads=[r_p], writes=[r_p])
        for half in range(2):
            pt, r_pt = self.bank()
            ptb = pt.bitcast(BF16)
            for g4 in range(4):
                g = half * 4 + g4
                fw.op("pe", lambda e: e.transpose(ptb[:, g4 * 128:(g4 + 1) * 128], wsb[:, g, :], ident),
                      reads=[r_p, r_c], writes=[r_pt], inc=(g4 == 3))
            fw.op("dve", lambda e: e.tensor_copy(out=wsT[:, half * 4:(half + 1) * 4, :], in_=ptb[:, 0:512].rearrange("p (g n) -> p g n", g=4)),
                  reads=[r_pt], writes=[r_p])
        NB = 2
        ut = [self.carve(16 * 512 * 2, BF16).rearrange("p (c t) -> p c t", c=16) for _ in range(NB)]
        za = [self.carve(16 * 512 * 2, BF16).rearrange("p (c t) -> p c t", c=16) for _ in range(NB)]
        br = [self.carve(16 * 512 * 2, BF16).rearrange("p (c t) -> p c t", c=16) for _ in range(NB)]
        r_ut = [Res("ut0"), Res("ut1")]
        r_br = [Res("br0"), Res("br1")]
        gv = [self.carve(4096, BF16) for _ in range(2)]
        r_gv = [Res("gv0"), Res("gv1")]
        tmpf = [self.carve(8192, F32) for _ in range(2)]
        r_tmp = [Res("tmpf0"), Res("tmpf1")]
        svn = [self.carve(4096, BF16) for _ in range(2)]
        r_svn = [Res("svn0"), Res("svn1")]
        junk = self.carve(4096, BF16)
        r_junk = Res("junk")
        st = [self.carve(64, F32) for _ in range(2)]
        r_st = [Res("st0"), Res("st1")]
        t1 = [self.carve(2048, F32) for _ in range(2)]
        r_t1 = [Res("t1a"), Res("t1b")]
        cnt = dict(k1=0)

        def load_tg(tg):
            b_ = tg % NB
            fw.dma("sp", out=ut[b_], in_=self.UT[:, tg * 512:(tg + 1) * 512].rearrange("(c p) t -> p c t", p=128), writes=[r_ut[b_]])
            fw.dma("sp", out=za[b_], in_=self.ZAT[:, tg * 512:(tg + 1) * 512].rearrange("(c p) t -> p c t", p=128), writes=[r_ut[b_]])
            fw.op("pool", lambda e: e.tensor_tensor(out=ut[b_], in0=ut[b_], in1=za[b_], op=ALU.mult), reads=[r_ut[b_]], writes=[r_ut[b_]])

        def stage_ln(n):
            s = n % 2
            fw.dma("sp", out=gv[s], in_=self.GV[n * 128:(n + 1) * 128, :], writes=[r_gv[s]])
            fw.op("act", lambda e: e.activation(out=junk, in_=gv[s], func=AF.Square, accum_out=st[s][:, 0:1]),
                  reads=[r_gv[s]], writes=[r_junk, r_st[s]])
            fw.op("dve", lambda e: e.reduce_sum(out=st[s][:, 1:2], in_=gv[s], axis=AX.X), reads=[r_gv[s]], writes=[r_st[s]])
            fw.op("dve", lambda e: e.tensor_scalar(out=st[s][:, 2:3], in0=st[s][:, 1:2], scalar1=1.0 / D, scalar2=None, op0=ALU.mult),
                  reads=[r_st[s]], writes=[r_st[s]])
            fw.op("dve", lambda e: e.tensor_tensor(out=st[s][:, 3:4], in0=st[s][:, 2:3], in1=st[s][:, 2:3], op=ALU.mult),
                  reads=[r_st[s]], writes=[r_st[s]])
            fw.op("dve", lambda e: e.scalar_tensor_tensor(out=st[s][:, 4:5], in0=st[s][:, 0:1], scalar=1.0 / D, in1=st[s][:, 3:4],
                                                          op0=ALU.mult, op1=ALU.subtract),
                  reads=[r_st[s]], writes=[r_st[s]])
            self.rstd_from(st[s][:, 4:5], st[s][:, 6:7], st[s][:, 5:6], 1, r_st[s])
            fw.op("dve", lambda e: e.tensor_scalar(out=tmpf[s], in0=gv[s], scalar1=st[s][:, 2:3], scalar2=st[s][:, 6:7],
                                                   op0=ALU.subtract, op1=ALU.mult),
                  reads=[r_gv[s], r_st[s]], writes=[r_tmp[s]])
            fw.op("pool", lambda e: e.tensor_tensor(out=tmpf[s], in0=tmpf[s], in1=lng, op=ALU.mult), reads=[r_tmp[s], r_p], writes=[r_tmp[s]])
            fw.op("dve", lambda e: e.tensor_tensor(out=svn[s], in0=tmpf[s], in1=lnb, op=ALU.add), reads=[r_tmp[s], r_p], writes=[r_svn[s]])

        def stage_mm(n):
            s = n % 2
            tg, nn = n // 4, n % 4
            b_ = tg % NB
            for q4 in range(4):
                ps, r_ps = self.bank()
                for c4 in range(4):
                    cc = q4 * 4 + c4
                    fw.op("pe", lambda e: e.matmul(ps[:, c4 * 128:(c4 + 1) * 128], svn[s][:, cc * 128:(cc + 1) * 128], wsT[:, cc // 2, :],
                                                   start=True, stop=True),
                          reads=[r_svn[s], r_p], writes=[r_ps], inc=(c4 == 3))
                k = cnt["k1"] % 2
                cnt["k1"] += 1
                t1v = t1[k].rearrange("p (c n) -> p c n", c=4)
                fw.op("dve", lambda e: e.tensor_tensor(out=t1v, in0=ps.rearrange("p (c n) -> p c n", c=4), in1=bsf[:, q4 * 4:(q4 + 1) * 4, :], op=ALU.add),
                      reads=[r_ps, r_p], writes=[r_t1[k]])
                fw.op("pool", lambda e: e.tensor_tensor(out=br[b_][:, q4 * 4:(q4 + 1) * 4, nn * 128:(nn + 1) * 128], in0=t1v,
                                                        in1=ut[b_][:, q4 * 4:(q4 + 1) * 4, nn * 128:(nn + 1) * 128], op=ALU.mult),
                      reads=[r_t1[k], r_ut[b_]], writes=[r_br[b_]])
            if nn == 3:
                fw.dma("sp", out=self.BRT[0][:, tg * 512:(tg + 1) * 512].rearrange("(c p) t -> p c t", p=128), in_=br[b_], reads=[r_br[b_]])

        load_tg(0)
        stage_ln(0)
        for n in range(16):
            if n + 1 < 16:
                if (n + 1) % 4 == 0:
                    load_tg((n + 1) // 4)
                stage_ln(n + 1)
            stage_mm(n)

    def phase_mixer_c(self, l):
        fw = self.fw
        lam_init = 0.8 - 0.6 * math.exp(-0.3 * l)
        cf, cb, r_c = self.load_ident()
        ident = cb[:, 0, :]
        r_p = Res("c_params")
        lv = self.carve(512 * 4, F32).rearrange("p (k n) -> p k n", k=4)
        lt = self.carve(256 * 4, F32).rearrange("p (k n) -> p k n", k=2)
        sc = self.carve(64, F32)
        fw.dma("sp", out=lv, in_=self.diff_lambda[l].partition_broadcast(128), writes=[r_p])
        lvv = lv.rearrange("p (a b) n -> p a b n", b=2)
        fw.op("dve", lambda e: e.tensor_tensor(out=lt, in0=lvv[:, :, 0, :], in1=lvv[:, :, 1, :], op=ALU.mult), reads=[r_p], writes=[r_p])
        fw.op("dve", lambda e: e.reduce_sum(out=sc[:, 0:2], in_=lt, axis=AX.X), reads=[r_p], writes=[r_p])
        fw.op("act", lambda e: e.activation(out=sc[:, 2:4], in_=sc[:, 0:2], func=AF.Exp), reads=[r_p], writes=[r_p])
        fw.op("dve", lambda e: e.scalar_tensor_tensor(out=sc[:, 4:5], in0=sc[:, 3:4], scalar=-lam_init, in1=sc[:, 2:3], op0=ALU.add, op1=ALU.subtract),
              reads=[r_p], writes=[r_p])
        nlam = sc[:, 4:5]
        onec = sc[:, 5:6]
        fw.op("dve", lambda e: e.memset(onec, 1.0), writes=[r_p])
        dnb = self.carve(256 * 4, F32)
        fw.dma("sp", out=dnb, in_=self.diff_norm[l].partition_broadcast(128), writes=[r_p])
        fw.op("dve", lambda e: e.tensor_scalar(out=dnb, in0=dnb, scalar1=1.0 - lam_init, scalar2=None, op0=ALU.mult), reads=[r_p], writes=[r_p])
        bfar = self.carve(64, F32)
        fw.dma("sp", out=bfar[:, 0:16], in_=self.bias_far.rearrange("h s -> (h s)").partition_broadcast(128), writes=[r_p])
        qT = [self.carve(2 * 2048 * 2, BF16).rearrange("p (c t) -> p c t", c=2) for _ in range(2)]
        kT = [self.carve(2 * 2048 * 2, BF16).rearrange("p (c t) -> p c t", c=2) for _ in range(2)]
        vv = [self.carve(16 * 272 * 2, BF16).rearrange("p (k e) -> p k e", k=16) for _ in range(2)]
        bt = [self.carve(3 * 128 * 4, F32).rearrange("p (k n) -> p k n", k=3) for _ in range(2)]
        bp = [self.carve(6 * 512 * 4, F32).rearrange("p (k n) -> p k n", k=6) for _ in range(2)]
        r_h = [Res("hd0"), Res("hd1")]
        for i in range(2):
            fw.op("pool", lambda e: e.memset(vv[i][:, :, 256:272], 0.0), writes=[r_h[i]])
            fw.op("pool", lambda e: e.memset(vv[i][:, :, 256:257], 1.0), writes=[r_h[i]])
        zc = [self.carve(2 * 512 * 2, BF16).rearrange("p (c t) -> p c t", c=2) for _ in range(2)]
        r_zc = [Res("zc0"), Res("zc1")]
        brt = [self.carve(2 * 512 * 2, BF16).rearrange("p (c t) -> p c t", c=2) for _ in range(2)]
        r_brt = [Res("brt0"), Res("brt1")]
        NPT = 4
        pT = [self.carve(512 * 2, BF16) for _ in range(NPT)]
        r_pT = [Res(f"pT{i}") for i in range(NPT)]
        tmpb = [self.carve(512 * 4, F32) for _ in range(3)]
        r_tmpb = [Res(f"tmpb{i}") for i in range(3)]
        o1 = self.carve(4 * 256 * 4, F32).rearrange("p (q e) -> p q e", q=4)
        r_o1 = Res("o1")
        oc = [self.carve(256 * 4, F32) for _ in range(8)]
        r_oc = [Res(f"oc{i}") for i in range(8)]
        raw = [self.carve(260 * 4, F32) for _ in range(8)]
        r_raw = [Res(f"raw{i}") for i in range(8)]
        yb = [self.carve(256 * 2, BF16) for _ in range(8)]
        r_yb = [Res(f"yb{i}") for i in range(8)]
        junk = self.carve(256 * 4, F32)
        r_junk = Res("junk")
        st = [self.carve(64, F32) for _ in range(2)]
        r_st = [Res("st0"), Res("st1")]
        st2 = [self.carve(64, F32) for _ in range(2)]
        r_st2 = [Res("st20"), Res("st21")]
        acc_banks = [0, 1, 2, 3]
        s_banks = [4, 5, 6]
        tp_bank = 7
        NPT_ = NPT
        state = dict(tbi=0, yi=0, loaded_h=-1, loaded_zc=-1)
        nheads = getattr(self, 'c_heads', 8)
        steps = [(h, qg, c, kt) for h in range(nheads) for qg in range(4) for c in range(2) for kt in range(16)]

        def load_head(h):
            hb = h % 2
            if state["loaded_h"] < h and h < nheads:
                state["loaded_h"] = h
                fw.dma("sp", out=qT[hb], in_=self.QCT[h * 256:(h + 1) * 256, :].rearrange("(c p) t -> p c t", p=128), writes=[r_h[hb]])
                fw.dma("sp", out=kT[hb], in_=self.KCT[h * 256:(h + 1) * 256, :].rearrange("(c p) t -> p c t", p=128), writes=[r_h[hb]])
                fw.dma("sp", out=vv[hb][:, :, 0:256], in_=self.VC[:, h * 256:(h + 1) * 256].rearrange("(k p) e -> p k e", p=128), writes=[r_h[hb]])
                fw.dma("sp", out=bp[hb], in_=self.bias_pat[h].rearrange("k p n -> p k n"), writes=[r_h[hb]])

        def load_zc(g):
            if state["loaded_zc"] < g and g < nheads * 4:
                state["loaded_zc"] = g
                h, qg = g // 4, g % 4
                zb = g % 2
                fw.dma("sp", out=zc[zb], in_=self.ZCT[h * 256:(h + 1) * 256, qg * 512:(qg + 1) * 512].rearrange("(c p) t -> p c t", p=128),
                       writes=[r_zc[zb]])

        def ensure_loads(h, qg):
            load_head(h)
            load_zc(h * 4 + qg)

        def emit_qk(i):
            h, qg, c, kt = steps[i]
            hb = h % 2
            ensure_loads(h, qg)
            sb_ = s_banks[i % 3]
            ps, r_ps = self.psum[sb_][:, :], self.ps_res[sb_]
            fw.op("pe", lambda e: e.matmul(ps, kT[hb][:, c, kt * 128:(kt + 1) * 128], qT[hb][:, c, qg * 512:(qg + 1) * 512],
                                           start=True, stop=True), reads=[r_h[hb]], writes=[r_ps])

        def emit_exp(i):
            h, qg, c, kt = steps[i]
            hb = h % 2
            sb_ = s_banks[i % 3]
            ps, r_ps = self.psum[sb_][:, :], self.ps_res[sb_]
            pt_ = i % NPT_
            dl = [kt - (qg * 4 + j) for j in range(4)]
            if all(d_ >= 2 for d_ in dl) or all(d_ <= -2 for d_ in dl):
                side = 1 if dl[0] > 0 else 0
                fw.op("act", lambda e: e.activation(out=pT[pt_], in_=ps, func=AF.Exp, bias=bfar[:, h * 2 + side:h * 2 + side + 1]),
                      reads=[r_ps, r_p], writes=[r_pT[pt_]])
            else:
                tb = state["tbi"] % 3
                state["tbi"] += 1
                p_ = (kt - 4 * qg) + 1
                fw.op("dve", lambda e: e.tensor_tensor(out=tmpb[tb], in0=ps, in1=bp[hb][:, p_, :], op=ALU.add),
                      reads=[r_ps, r_h[hb]], writes=[r_tmpb[tb]])
                fw.op("act", lambda e: e.activation(out=pT[pt_], in_=tmpb[tb], func=AF.Exp), reads=[r_tmpb[tb]], writes=[r_pT[pt_]])

        def emit_pv(i):
            h, qg, c, kt = steps[i]
            hb = h % 2
            pt_ = i % NPT_
            for j in range(4):
                ab = acc_banks[j]
                fw.op("pe", lambda e: e.matmul(self.psum[ab][:, 0:264], pT[pt_][:, j * 128:(j + 1) * 128], vv[hb][:, kt, 0:264],
                                               start=(kt == 0), stop=(kt == 15)),
                      reads=[r_pT[pt_], r_h[hb]], writes=[self.ps_res[ab]], inc=(j == 3))

        deferred = []

        def emit_epilogue(i):
            h, qg, c, kt = steps[i]
            zb = (h * 4 + qg) % 2
            g_ = state["yi"] % 2
            state["yi"] += 1
            stg_, r_stg_ = st[g_], r_st[g_]
            raws = [raw[g_ * 4 + j] for j in range(4)]
            r_raws = [r_raw[g_ * 4 + j] for j in range(4)]
            for j in range(4):
                pa, r_pa = self.psum[acc_banks[j]], self.ps_res[acc_banks[j]]
                fw.op("dve", lambda e: e.tensor_copy(out=raws[j][:, 0:257], in_=pa[:, 0:257]), reads=[r_pa], writes=[r_raws[j]])
            ocs = [oc[g_ * 4 + j] for j in range(4)]
            r_ocs = [r_oc[g_ * 4 + j] for j in range(4)]

            def p_recip():
                for j in range(4):
                    fw.op("dve", lambda e: e.reciprocal(out=stg_[:, j:j + 1], in_=raws[j][:, 256:257]), reads=[r_raws[j]], writes=[r_stg_])

            def p_o1():
                for j in range(4):
                    fw.op("pool", lambda e: e.tensor_scalar(out=o1[:, j, :], in0=raws[j][:, 0:256], scalar1=stg_[:, j:j + 1], scalar2=onec, op0=ALU.mult, op1=ALU.mult),
                          reads=[r_raws[j], r_stg_], writes=[r_o1])

            def p_oc():
                for j in range(4):
                    fw.op("pool", lambda e: e.tensor_scalar(out=ocs[j], in0=raws[j][:, 0:256], scalar1=stg_[:, j:j + 1], scalar2=nlam, op0=ALU.mult, op1=ALU.mult),
                          reads=[r_raws[j], r_stg_, r_p], writes=[r_ocs[j]])
                    fw.op("pool", lambda e: e.tensor_tensor(out=ocs[j], in0=ocs[j], in1=o1[:, j, :], op=ALU.add), reads=[r_ocs[j], r_o1], writes=[r_ocs[j]])

            def p_ss():
                fw.op("pool", lambda e: e.memset(stg2_[:, 0:4], 0.0), writes=[r_stg2_])
                for j in range(4):
                    fw.op("act", lambda e: e.activation(out=junk, in_=ocs[j], func=AF.Square, accum_out=stg2_[:, j:j + 1]),
                          reads=[r_ocs[j]], writes=[r_junk, r_stg2_])

            def p_rstd():
                self.rstd_from(stg2_[:, 0:4], stg2_[:, 4:8], stg2_[:, 8:12], 256, r_stg2_)

            def p_yb():
                for j in range(4):
                    fw.op("pool", lambda e: e.tensor_scalar(out=ocs[j], in0=ocs[j], scalar1=stg2_[:, 4 + j:5 + j], scalar2=onec, op0=ALU.mult, op1=ALU.mult),
                          reads=[r_ocs[j], r_stg2_], writes=[r_ocs[j]])
                    fw.op("pool", lambda e: e.tensor_tensor(out=yb[g_ * 4 + j], in0=ocs[j], in1=dnb, op=ALU.mult), reads=[r_ocs[j], r_p], writes=[r_yb[g_ * 4 + j]])

            def p_tp():
                ptp, r_tp = self.psum[tp_bank][:, :].bitcast(BF16), self.ps_res[tp_bank]
                for j in range(4):
                    for ec in range(2):
                        fw.op("pe", lambda e: e.transpose(ptp[:, (j * 2 + ec) * 128:(j * 2 + ec + 1) * 128], yb[g_ * 4 + j][:, ec * 128:(ec + 1) * 128], ident),
                              reads=[r_yb[g_ * 4 + j], r_c], writes=[r_tp], inc=(j == 3 and ec == 1))

            def p_brt():
                ptp, r_tp = self.psum[tp_bank][:, :].bitcast(BF16), self.ps_res[tp_bank]
                for j in range(4):
                    fw.op("dve", lambda e: e.tensor_tensor(out=brt[zb][:, :, j * 128:(j + 1) * 128],
                                                           in0=ptp[:, j * 256:(j + 1) * 256].rearrange("p (c t) -> p c t", c=2),
                                                           in1=zc[zb][:, :, j * 128:(j + 1) * 128], op=ALU.mult),
                          reads=[r_tp, r_zc[zb]], writes=[r_brt[zb]])
                fw.dma("sp", out=self.BRT[2][h * 256:(h + 1) * 256, qg * 512:(qg + 1) * 512].rearrange("(c p) t -> p c t", p=128), in_=brt[zb],
                       reads=[r_brt[zb]])

            stg2_, r_stg2_ = st2[g_], r_st2[g_]
            defer(i + 1, p_recip)
            if c == 0:
                defer(i + 2, p_o1)
                return
            defer(i + 2, p_oc)
            defer(i + 8, p_ss)
            defer(i + 11, p_rstd)
            defer(i + 12, p_yb)
            defer(i + 19, p_tp)
            defer(i + 21, p_brt)

        def defer(at, fn):
            deferred.append((at, fn))
            deferred.sort(key=lambda t: t[0])

        if getattr(self, 'c_stop', 0) == 1:
            return
        SKEW = 2
        n = len(steps)
        for i in range(min(SKEW, n)):
            emit_qk(i)
        for i in range(n):
            emit_exp(i)
            if i + SKEW < n:
                emit_qk(i + SKEW)
            emit_pv(i)
            h_, qg_, c_, kt_ = steps[i]
            if qg_ == 0 and c_ == 0 and kt_ == 2:
                load_head(h_ + 1)
            if c_ == 1 and kt_ == 8:
                load_zc(h_ * 4 + qg_ + 1)
            if steps[i][3] == 15:
                emit_epilogue(i)
            while deferred and deferred[0][0] <= i:
                deferred.pop(0)[1]()
        while deferred:
            deferred.pop(0)[1]()


    def phase_mixer_b(self, l):
        fw = self.fw
        cf, cb, r_c = self.load_ident()
        ident = cb[:, 0, :]
        r_p = Res("b_params")
        waf = self.carve(2 * 1024 * 4, F32).rearrange("p (k n) -> p k n", k=2)
        wab = self.carve(2 * 1024 * 2, BF16).rearrange("p (k n) -> p k n", k=2)
        lrf = self.carve(2048 * 4, F32)
        lrb = self.carve(2048 * 2, BF16)
        gnb = self.carve(512 * 4, F32)
        fw.op("dve", lambda e: e.memset(waf[0:64], 0.0), writes=[r_p])
        fw.op("dve", lambda e: e.memset(lrf[32:64], 1.0), writes=[r_p])
        fw.dma("sp", out=waf[0:16, 0, :], in_=self.gla_wa2[l, 0], writes=[r_p])
        fw.dma("sp", out=waf[16:32, 1, :], in_=self.gla_wa2[l, 1], writes=[r_p])
        fw.dma("sp", out=waf[32:33, :, :], in_=self.gla_ba[l:l + 1], writes=[r_p])
        fw.dma("sp", out=lrf[0:32, :], in_=self.LRT, writes=[r_p])
        fw.dma("sp", out=gnb, in_=self.gla_norm[l].partition_broadcast(128), writes=[r_p])
        fw.op("dve", lambda e: e.tensor_copy(out=wab[0:64], in_=waf[0:64]), reads=[r_p], writes=[r_p])
        fw.op("dve", lambda e: e.tensor_copy(out=lrb[0:64], in_=lrf[0:64]), reads=[r_p], writes=[r_p])
        Sf = self.carve(4 * 2 * 512 * 4, F32).rearrange("p (h c e) -> p h c e", h=4, c=2)
        Sb = self.carve(4 * 2 * 512 * 2, BF16).rearrange("p (h c e) -> p h c e", h=4, c=2)
        r_S = [[Res(f"S{h}{c}") for c in range(2)] for h in range(4)]
        r_Sb = [[Res(f"Sb{h}{c}") for c in range(2)] for h in range(4)]

        def c3(n, dt_, k):
            return self.carve(n * (4 if dt_ == F32 else 2), dt_).rearrange("p (k n) -> p k n", k=k)

        qT = [c3(1024, BF16, 8) for _ in range(2)]
        kT = [c3(1024, BF16, 8) for _ in range(2)]
        kk = [self.carve(1024 * 2, BF16) for _ in range(2)]
        vt = [self.carve(2048 * 2, BF16) for _ in range(2)]
        r_in = [Res("in0"), Res("in1")]
        r_kk = [Res("kk0"), Res("kk1")]
        e1 = self.carve(1024 * 4, F32)
        r_e1 = Res("e1")
        gneg = self.carve(1024 * 2, BF16)
        r_g = Res("gneg")
        E1 = c3(1024, F32, 8)
        E2 = c3(1024, F32, 8)
        E3 = c3(1024, F32, 8)
        FF = self.carve(1024 * 4, F32)
        r_E1, r_E2, r_E3, r_FF = Res("E1"), Res("E2"), Res("E3"), Res("FF")
        qg = [c3(1024, BF16, 8) for _ in range(2)]
        kg = [c3(1024, BF16, 8) for _ in range(2)]
        qi = [c3(1024, BF16, 8) for _ in range(2)]
        ks = [self.carve(1024 * 2, BF16) for _ in range(2)]
        dec = [self.carve(64, F32).rearrange("p (k n) -> p k n", k=8) for _ in range(2)]
        r_qg = [Res("qg0"), Res("qg1")]; r_kg = [Res("kg0"), Res("kg1")]; r_qi = [Res("qi0"), Res("qi1")]; r_ks = [Res("ks0"), Res("ks1")]; r_dec = [Res("dec0"), Res("dec1")]
        sT = [self.carve(128 * 2, BF16) for _ in range(4)]
        r_sT = [Res(f"sT{i}") for i in range(4)]
        oft = [self.carve(2048 * 4, F32).rearrange("p (h e) -> p h e", h=4) for _ in range(2)]
        r_oft = [Res("oft0"), Res("oft1")]
        osum = [self.carve(512 * 4, F32) for _ in range(4)]
        r_os = [Res(f"os{i}") for i in range(4)]
        pend2, pend3 = [], []
        junk = self.carve(512 * 4, F32)
        r_junk = Res("junk")
        st = [self.carve(64, F32) for _ in range(4)]
        r_st = [Res(f"st{i}") for i in range(4)]
        yb = [self.carve(512 * 2, BF16) for _ in range(4)]
        r_yb = [Res(f"yb{i}") for i in range(4)]
        zball = [c3(2048, BF16, 16) for _ in range(2)]
        r_zball = [Res("zball0"), Res("zball1")]
        brt = [c3(512, BF16, 4) for _ in range(4)]
        r_brt = [Res(f"brt{i}") for i in range(4)]
        P = self.psum
        R = self.ps_res
        r_OF = [Res(f"OF{t}") for t in range(16)]
        A = (0, 1)
        C = 2
        Db = (3, 4)
        Eb = (5, 6)
        TP = 7
        nt = getattr(self, 'b_tiles', 16)
        steps = []
        for dr in range(2):
            tiles = list(range(16)) if dr == 0 else list(range(15, -1, -1))
            for ti, T in enumerate(tiles[:nt]):
                steps.append((dr, ti, T))
        cnt = dict(e=0, s=0)

        def prologue(si):
            dr, ti, T = steps[si]
            b_ = si % 2
            OPc, OPs, OPd = (cb[:, 1, :], cb[:, 3, :], cb[:, 5, :]) if dr == 0 else (cb[:, 2, :], cb[:, 4, :], cb[:, 6, :])
            tsl = slice(T * 128, (T + 1) * 128)
            fw.dma("sp", out=qT[b_], in_=self.QBT[:, tsl].rearrange("(c p) t -> p c t", p=128), writes=[r_in[b_]])
            fw.dma("sp", out=kT[b_], in_=self.KBT[:, tsl].rearrange("(c p) t -> p c t", p=128), writes=[r_in[b_]])
            fw.dma("sp", out=vt[b_], in_=self.VB[tsl, :], writes=[r_in[b_]])
            ktp = P[A[0]][:, :].bitcast(BF16)
            for dk in range(8):
                fw.op("pe", lambda e: e.transpose(ktp[:, dk * 128:(dk + 1) * 128], kT[b_][:, dk, :], ident),
                      reads=[r_in[b_], r_c], writes=[R[A[0]]], inc=(dk == 7))
            fw.op("dve", lambda e: e.tensor_copy(out=kk[b_], in_=ktp), reads=[R[A[0]]], writes=[r_kk[b_]])
            for hf in range(2):
                fw.op("pe", lambda e: e.matmul(P[A[hf]][:, :], lrb[0:64, tsl], wab[0:64, dr, hf * 512:(hf + 1) * 512], start=True, stop=True),
                      reads=[r_p], writes=[R[A[hf]]])
                fw.op("act", lambda e: e.activation(out=e1[:, hf * 512:(hf + 1) * 512], in_=P[A[hf]][:, :], func=AF.Exp, scale=-1.0),
                      reads=[R[A[hf]]], writes=[r_e1])
            fw.op("act", lambda e: e.activation(out=gneg, in_=e1, func=AF.Ln, bias=1.0, scale=1.0), reads=[r_e1], writes=[r_g])
            yield
            for hf in range(2):
                for d4 in range(4):
                    dk = hf * 4 + d4
                    fw.op("pe", lambda e: e.matmul(P[A[hf]][:, d4 * 128:(d4 + 1) * 128], gneg[:, dk * 128:(dk + 1) * 128], OPd, start=True, stop=True),
                          reads=[r_g, r_c], writes=[R[A[hf]]], inc=(d4 == 3))
                src = P[A[hf]][:, :].rearrange("p (k n) -> p k n", k=4)
                fw.op("act", lambda e: e.activation(out=E1[:, hf * 4:(hf + 1) * 4, :], in_=src, func=AF.Exp, scale=-1.0 / 16), reads=[R[A[hf]]], writes=[r_E1])
                fw.op("act", lambda e: e.activation(out=E2[:, hf * 4:(hf + 1) * 4, :], in_=src, func=AF.Exp, scale=1.0 / 16), reads=[R[A[hf]]], writes=[r_E2])
            fw.op("dve", lambda e: e.tensor_tensor(out=qg[b_], in0=qT[b_], in1=E1, op=ALU.mult), reads=[r_in[b_], r_E1], writes=[r_qg[b_]])
            fw.op("pool", lambda e: e.tensor_tensor(out=kg[b_], in0=kT[b_], in1=E2, op=ALU.mult), reads=[r_in[b_], r_E2], writes=[r_kg[b_]])
            yield
            lastc = 127 if dr == 0 else 0
            for hf in range(2):
                for d4 in range(4):
                    dk = hf * 4 + d4
                    fw.op("pe", lambda e: e.matmul(P[A[hf]][:, d4 * 128:(d4 + 1) * 128], gneg[:, dk * 128:(dk + 1) * 128], OPc, start=True, stop=True),
                          reads=[r_g, r_c], writes=[R[A[hf]]], inc=(d4 == 3))
                src = P[A[hf]][:, :].rearrange("p (k n) -> p k n", k=4)
                fw.op("act", lambda e: e.activation(out=E3[:, hf * 4:(hf + 1) * 4, :], in_=src, func=AF.Exp, scale=-1.0 / 16), reads=[R[A[hf]]], writes=[r_E3])
                srcd = P[A[hf]][:, :].rearrange("p (k n) -> p k n", k=4)[:, :, lastc:lastc + 1]
                fw.op("act", lambda e: e.activation(out=dec[b_][:, hf * 4:(hf + 1) * 4, 0:1], in_=srcd, func=AF.Exp, scale=-1.0 / 16),
                      reads=[R[A[hf]]], writes=[r_dec[b_]])
            fw.op("dve", lambda e: e.tensor_tensor(out=qi[b_], in0=qT[b_], in1=E3, op=ALU.mult), reads=[r_in[b_], r_E3], writes=[r_qi[b_]])
            yield
            for hf in range(2):
                fw.op("pe", lambda e: e.matmul(P[A[hf]][:, :], OPs, gneg[:, hf * 512:(hf + 1) * 512], start=True, stop=True),
                      reads=[r_g, r_c], writes=[R[A[hf]]])
                fw.op("act", lambda e: e.activation(out=FF[:, hf * 512:(hf + 1) * 512], in_=P[A[hf]][:, :], func=AF.Exp, scale=-1.0 / 16),
                      reads=[R[A[hf]]], writes=[r_FF])
            for hf in range(2):
                fw.op("pool", lambda e: e.tensor_tensor(out=ks[b_][:, hf * 512:(hf + 1) * 512], in0=kk[b_][:, hf * 512:(hf + 1) * 512], in1=FF[:, hf * 512:(hf + 1) * 512], op=ALU.mult),
                      reads=[r_kk[b_], r_FF], writes=[r_ks[b_]])
            yield

        def heads(si):
            dr, ti, T = steps[si]
            b_ = si % 2
            tsl = slice(T * 128, (T + 1) * 128)
            maskf = cf[:, 1, :] if dr == 0 else cf[:, 2, :]
            if ti == 0:
                for h in range(4):
                    for c in range(2):
                        fw.op("dve", lambda e: e.memset(Sf[:, h, c, :], 0.0), writes=[r_S[h][c]])
                        fw.op("pool", lambda e: e.memset(Sb[:, h, c, :], 0.0), writes=[r_Sb[h][c]])
            if dr == 1:
                fw.dma("sp", out=oft[b_], in_=self.OF[tsl, :].rearrange("p (h e) -> p h e", h=4), reads=[r_OF[T]], writes=[r_oft[b_]])
                fw.dma("sp", out=zball[b_], in_=self.ZBT[:, tsl].rearrange("(c p) t -> p c t", p=128), writes=[r_zball[b_]])
            for hp in range(2):
                hs = (2 * hp, 2 * hp + 1)
                for k2, h in enumerate(hs):
                    for dc in range(2):
                        fw.op("pe", lambda e: e.matmul(P[C][:, k2 * 128:(k2 + 1) * 128], kg[b_][:, h * 2 + dc, :], qg[b_][:, h * 2 + dc, :], start=(dc == 0), stop=(dc == 1)),
                              reads=[r_qg[b_], r_kg[b_]], writes=[R[C]], inc=(dc == 1))
                sidx = []
                for k2, h in enumerate(hs):
                    s_ = cnt["s"] % 4
                    cnt["s"] += 1
                    sidx.append(s_)
                    fw.op("dve", lambda e: e.tensor_tensor(out=sT[s_], in0=P[C][:, k2 * 128:(k2 + 1) * 128], in1=maskf, op=ALU.mult), reads=[R[C], r_c], writes=[r_sT[s_]])
                while pend2:
                    pend2.pop(0)()
                for k2, h in enumerate(hs):
                    D_ = Db[k2]
                    for dc in range(2):
                        fw.op("pe", lambda e: e.matmul(P[D_][:, :], qi[b_][:, h * 2 + dc, :], Sb[:, h, dc, :], start=(dc == 0), stop=False),
                              reads=[r_qi[b_], r_Sb[h][dc]], writes=[R[D_]], inc=False)
                    fw.op("pe", lambda e: e.matmul(P[D_][:, :], sT[sidx[k2]], vt[b_][:, h * 512:(h + 1) * 512], start=False, stop=True),
                          reads=[r_sT[sidx[k2]], r_in[b_]], writes=[R[D_]])
                while pend3:
                    pend3.pop(0)()
                for k2, h in enumerate(hs):
                    for dc in range(2):
                        E_ = Eb[cnt["e"] % 2]
                        cnt["e"] += 1
                        fw.op("pe", lambda e: e.matmul(P[E_][:, :], ks[b_][:, (h * 2 + dc) * 128:(h * 2 + dc + 1) * 128], vt[b_][:, h * 512:(h + 1) * 512],
                                                       start=True, stop=True), reads=[r_ks[b_], r_in[b_]], writes=[R[E_]])
                        fw.op("dve", lambda e: e.scalar_tensor_tensor(out=Sf[:, h, dc, :], in0=Sf[:, h, dc, :], scalar=dec[b_][:, h * 2 + dc, 0:1],
                                                                      in1=P[E_][:, :], op0=ALU.mult, op1=ALU.add),
                              reads=[r_S[h][dc], r_dec[b_], R[E_]], writes=[r_S[h][dc]])
                        fw.op("act", lambda e: e.activation(out=Sb[:, h, dc, :], in_=Sf[:, h, dc, :], func=AF.Copy), reads=[r_S[h][dc]], writes=[r_Sb[h][dc]])
                yield
                for k2, h in enumerate(hs):
                    D_ = Db[k2]
                    if dr == 0:
                        fw.op("act", lambda e: e.activation(out=oft[b_][:, h, :], in_=P[D_][:, :], func=AF.Copy), reads=[R[D_]], writes=[r_oft[b_]])
                    else:
                        s_ = h % 4
                        fw.op("dve", lambda e: e.tensor_tensor(out=osum[s_], in0=P[D_][:, :], in1=oft[b_][:, h, :], op=ALU.add),
                              reads=[R[D_], r_oft[b_]], writes=[r_os[s_]])
                        fw.op("act", lambda e: e.activation(out=junk, in_=osum[s_], func=AF.Square, accum_out=st[s_][:, 0:1]),
                              reads=[r_os[s_]], writes=[r_junk, r_st[s_]])
                        self.rstd_from(st[s_][:, 0:1], st[s_][:, 2:3], st[s_][:, 1:2], 512, r_st[s_])

                        def part2(s_=s_, h=h, tsl=tsl, b_=b_):
                            fw.op("dve", lambda e: e.scalar_tensor_tensor(out=yb[s_], in0=osum[s_], scalar=st[s_][:, 2:3], in1=gnb, op0=ALU.mult, op1=ALU.mult),
                                  reads=[r_os[s_], r_st[s_], r_p], writes=[r_yb[s_]])

                        def part3(s_=s_, h=h, tsl=tsl, b_=b_):
                            ptp = P[TP][:, 0:256].bitcast(BF16)
                            for ec in range(4):
                                fw.op("pe", lambda e: e.transpose(ptp[:, ec * 128:(ec + 1) * 128], yb[s_][:, ec * 128:(ec + 1) * 128], ident),
                                      reads=[r_yb[s_], r_c], writes=[R[TP]], inc=(ec == 3))
                            fw.op("dve", lambda e: e.tensor_tensor(out=brt[s_], in0=ptp.rearrange("p (c t) -> p c t", c=4), in1=zball[b_][:, h * 4:(h + 1) * 4, :], op=ALU.mult),
                                  reads=[R[TP], r_zball[b_]], writes=[r_brt[s_]])
                            fw.dma("sp", out=self.BRT[1][h * 512:(h + 1) * 512, tsl].rearrange("(c p) t -> p c t", p=128), in_=brt[s_], reads=[r_brt[s_]])

                        pend2.append(part2)
                        pend3.append(part3)
                if hp == 0:
                    yield
            if dr == 0:
                fw.dma("sp", out=self.OF[tsl, :].rearrange("p (h e) -> p h e", h=4), in_=oft[b_], reads=[r_oft[b_]], writes=[r_OF[T]])
            yield

        for _ in prologue(0):
            pass
        for si in range(len(steps)):
            pg = prologue(si + 1) if si + 1 < len(steps) else iter(())
            for _ in heads(si):
                next(pg, None)
            for _ in pg:
                pass
        while pend2:
            pend2.pop(0)()
        while pend3:
            pend3.pop(0)()


    def phase_merge(self, l, mT, r_mT):
        fw = self.fw
        brt = [self.carve(16 * 512 * 2, BF16).rearrange("p (c t) -> p c t", c=16) for _ in range(3)]
        r_br = [Res(f"brt{i}") for i in range(3)]
        NW = 4
        wb = [self.carve(16 * 512 * 2, BF16).rearrange("p (c n) -> p c n", c=16) for _ in range(NW)]
        r_w = [Res(f"w{i}") for i in range(NW)]
        gt = [self.carve(3 * 512 * 2, BF16).rearrange("p (i t) -> p i t", i=3) for _ in range(2)]
        r_gt = [Res("gt0"), Res("gt1")]
        tt = [[self.carve(512 * 4, F32) for _ in range(3)] for _ in range(2)]
        r_tt = [[Res(f"tt{k}{i}") for i in range(3)] for k in range(2)]
        GTv = self.GT.rearrange("(i c p) t -> c p i t", i=3, p=128)
        wi = 0
        ei = 0
        for tg in range(4):
            tsl = slice(tg * 512, (tg + 1) * 512)
            for i in range(3):
                fw.dma("sp", out=brt[i], in_=self.BRT[i][:, tsl].rearrange("(c p) t -> p c t", p=128), writes=[r_br[i]])
            for dblk in range(4):
                ws = []
                for i in range(3):
                    w_ = wi % NW
                    wi += 1
                    ws.append(w_)
                    fw.dma("sp", out=wb[w_], in_=self.WB[i][:, dblk * 512:(dblk + 1) * 512].rearrange("(c p) n -> p c n", p=128), writes=[r_w[w_]])
                for j in range(4):
                    dsub = dblk * 4 + j
                    k = ei % 2
                    ei += 1
                    fw.dma("sp", out=gt[k], in_=GTv[dsub][:, :, tsl], writes=[r_gt[k]])
                    pss = []
                    for i in range(3):
                        ps, r_ps = self.bank()
                        pss.append((ps, r_ps))
                        for c in range(16):
                            fw.op("pe", lambda e: e.matmul(ps, wb[ws[i]][:, c, j * 128:(j + 1) * 128], brt[i][:, c, :], start=(c == 0), stop=(c == 15)),
                                  reads=[r_w[ws[i]], r_br[i]], writes=[r_ps], inc=(c == 15))
                    for i in range(3):
                        ps, r_ps = pss[i]
                        fw.op("dve", lambda e: e.tensor_tensor(out=tt[k][i], in0=ps, in1=gt[k][:, i, :], op=ALU.mult),
                              reads=[r_ps, r_gt[k]], writes=[r_tt[k][i]])
                    fw.op("pool", lambda e: e.tensor_tensor(out=tt[k][0], in0=tt[k][0], in1=tt[k][1], op=ALU.add),
                          reads=[r_tt[k][0], r_tt[k][1]], writes=[r_tt[k][0]])
                    fw.op("pool", lambda e: e.tensor_tensor(out=mT[:, dsub, tsl], in0=tt[k][0], in1=tt[k][2], op=ALU.add),
                          reads=[r_tt[k][0], r_tt[k][2]], writes=[r_mT])

    def phase_out(self, l, mT, r_mT, x_src, x_dst):
        fw = self.fw
        wo = self.carve(16 * 2048 * 2, BF16).rearrange("p (c n) -> p c n", c=16)
        r_wo = [Res(f"wo{i}") for i in range(4)]
        for nb in range(4):
            fw.dma("sp", out=wo[:, :, nb * 512:(nb + 1) * 512], in_=self.WO[:, nb * 512:(nb + 1) * 512].rearrange("(c p) n -> p c n", p=128),
                   writes=[r_wo[nb]])
        npb = self.carve(8192, F32)
        r_np = Res("npb")
        fw.dma("sp", out=npb, in_=self.norm_post[l].partition_broadcast(128), writes=[r_np])
        xt = [self.carve(8192, F32) for _ in range(2)]
        r_xt = [Res("xt0"), Res("xt1")]
        ot = [self.carve(8192, F32) for _ in range(2)]
        r_ot = [Res("ot0"), Res("ot1")]
        junk = self.carve(2048, F32)
        r_junk = Res("junk")
        st = [self.carve(64, F32) for _ in range(2)]
        r_st = [Res("st0"), Res("st1")]
        for t in range(NT):
            s = t % 2
            fw.dma("sp", out=xt[s], in_=x_src[t * 128:(t + 1) * 128, :], writes=[r_xt[s]])
            banks = []
            for nb in range(4):
                ps, r_ps = self.bank()
                banks.append((ps, r_ps))
                for c in range(16):
                    fw.op("pe", lambda e: e.matmul(ps, mT[:, c, t * 128:(t + 1) * 128], wo[:, c, nb * 512:(nb + 1) * 512], start=(c == 0), stop=(c == 15)),
                          reads=[r_mT, r_wo[nb]], writes=[r_ps], inc=(c == 15))
                fw.op("act", lambda e: e.activation(out=junk, in_=ps, func=AF.Square, accum_out=st[s][:, nb:nb + 1]),
                      reads=[r_ps], writes=[r_junk, r_st[s]])
            fw.op("dve", lambda e: e.reduce_sum(out=st[s][:, 4:5], in_=st[s][:, 0:4], axis=AX.X), reads=[r_st[s]], writes=[r_st[s]])
            self.rstd_from(st[s][:, 4:5], st[s][:, 6:7], st[s][:, 5:6], D, r_st[s])
            for nb in range(4):
                ps, r_ps = banks[nb]
                sl = slice(nb * 512, (nb + 1) * 512)
                fw.op("dve", lambda e: e.scalar_tensor_tensor(out=ot[s][:, sl], in0=ps, scalar=st[s][:, 6:7], in1=npb[:, sl], op0=ALU.mult, op1=ALU.mult),
                      reads=[r_ps, r_st[s], r_np], writes=[r_ot[s]])
            fw.op("pool", lambda e: e.tensor_tensor(out=ot[s], in0=ot[s], in1=xt[s], op=ALU.add), reads=[r_ot[s], r_xt[s]], writes=[r_ot[s]])
            fw.dma("sp", out=x_dst[t * 128:(t + 1) * 128, :], in_=ot[s], reads=[r_ot[s]])

    def setup_consts(self):
        fw = self.fw
        base = self.ARENA_F32 - 64
        self.eps_tile = self.arena[:, base:base + 1]
        self.r_const = Res("const")
        fw.op("dve", lambda e: e.memset(self.eps_tile, EPS), writes=[self.r_const])
        self.ARENA_LIMIT = base * 4

    def build(self):
        fw = self.fw
        self.carve_reset()
        self.setup_consts()
        fw.barrier()
        for l in range(self.depth):
            x_src = self.x if l == 0 else self.X1
            self.carve_reset()
            hT = self.carve(16 * 2048 * 2, BF16).rearrange("p (c t) -> p c t", c=16)
            hT_res = Res("hT")
            mark = self._carve
            self.phase_norm(l, x_src, hT, hT_res)
            fw.barrier()
            if "HT" in self.debug and l == self.depth - 1:
                HT = self.scratch("HT", [D, S])
                fw.dma("sp", out=HT.rearrange("(c p) t -> p c t", p=128), in_=hT, reads=[hT_res])
            if self.stop_after == ("norm", l):
                break
            self._carve = mark
            bm = self.carve(48 * 4, F32)
            fw.dma("sp", out=bm, in_=self.b_merge[l].rearrange("(k p) -> p k", p=128), writes=[self.r_const], slow=True)
            blocks = self.inproj_blocks(l, bm)
            if "only_blocks" in self.__dict__:
                blocks = [blocks[i] for i in self.only_blocks]
            side = []
            for i in range(3):
                for c0 in range(0, D, 512):
                    side.append((self.WB[i][:, c0:c0 + 512].rearrange("(c p) n -> p c n", p=128),
                                 self.w_branch[l, i][:, c0:c0 + 512].rearrange("(c p) n -> p c n", p=128)))
            for c0 in range(0, D, 512):
                side.append((self.WO[:, c0:c0 + 512].rearrange("(c p) n -> p c n", p=128),
                             self.w_out[l][:, c0:c0 + 512].rearrange("(c p) n -> p c n", p=128)))
            self.proj_blocks(hT, hT_res, blocks, side_dmas=side)
            fw.barrier()
            if self.stop_after == ("inproj", l):
                break
            for name, fn in (("a", self.phase_mixer_a), ("b", self.phase_mixer_b), ("c", self.phase_mixer_c)):
                if self.skip and name in self.skip:
                    continue
                self.carve_reset()
                fn(l)
                fw.barrier()
            if self.stop_after == ("mix", l):
                break
            self.carve_reset()
            mT = self.carve(16 * 2048 * 2, BF16).rearrange("p (c t) -> p c t", c=16)
            r_mT = Res("mT")
            mark = self._carve
            self.phase_merge(l, mT, r_mT)
            fw.barrier()
            if "MT" in self.debug and l == self.depth - 1:
                MT = self.scratch("MT", [D, S])
                fw.dma("sp", out=MT.rearrange("(c p) t -> p c t", p=128), in_=mT, reads=[r_mT])
                fw.barrier()
            self._carve = mark
            x_dst = self.out if l == self.depth - 1 else self.X1
            self.phase_out(l, mT, r_mT, x_src, x_dst)
            fw.barrier()
        fw.barrier()
        return self.nc


def _t5_buckets_np(rel):
    nb = 16
    max_exact = 8
    rel = np.asarray(rel, np.int32)
    ret = np.where(rel > 0, nb, 0).astype(np.int32)
    n = np.abs(rel).astype(np.int32)
    nf = np.maximum(n, 1).astype(np.float32)
    large = max_exact + (np.log(nf / np.float32(max_exact)) / np.float32(math.log(128 / max_exact))
                         * np.float32(nb - max_exact)).astype(np.int32)
    large = np.minimum(large, nb - 1)
    return ret + np.where(n < max_exact, n, large)


def _make_consts():
    c = np.zeros((8, 128, 128), np.float32)
    c[0] = np.eye(128, dtype=np.float32)
    i = np.arange(128)
    jj, ii = i[:, None], i[None, :]
    same = (jj >= 0) & (ii >= 0)
    le = (same & (jj <= ii)).astype(np.float32)
    ge = (same & (jj >= ii)).astype(np.float32)
    gt = (same & (jj > ii)).astype(np.float32)
    lt = (same & (jj < ii)).astype(np.float32)
    c[1], c[2], c[3], c[4] = le, ge, gt, lt
    reff = (same & (jj <= 63)).astype(np.float32)
    refb = (same & (jj >= 64)).astype(np.float32)
    c[5] = le - reff
    c[6] = ge - refb
    c[7] = 1.0
    return c


_CONSTS = _make_consts()
_kk = np.arange(128)[:, None]
_qq = np.arange(128)[None, :]
_BUCKET_TILES = np.stack([_t5_buckets_np(_kk - _qq + 128 * dlt) for dlt in (-1, 0, 1)])
_BUCKET_PAT = np.stack([np.concatenate([_t5_buckets_np(_kk - _qq + 128 * (d0 - j)) for j in range(4)], axis=1) for d0 in range(-1, 5)])


def make_shared_inputs(inp):
    f = lambda a: np.ascontiguousarray(np.asarray(a, dtype=np.float32))
    rb = f(inp["rel_bias"])
    shared = {k: f(inp[k]) for k in ("norm_pre", "w_in", "gmlp_ln_g", "gmlp_ln_b", "gmlp_ws", "gmlp_bs", "gla_wa2", "gla_ba",
                                      "gla_norm", "diff_lambda", "diff_norm", "w_branch", "w_merge", "b_merge", "w_out", "norm_post")}
    shared["bias_tiles"] = np.ascontiguousarray(np.transpose(rb[_BUCKET_TILES], (3, 0, 1, 2)))
    shared["bias_far"] = np.ascontiguousarray(np.stack([rb[15], rb[31]], axis=1))
    shared["bias_pat"] = np.ascontiguousarray(np.transpose(rb[_BUCKET_PAT], (3, 0, 1, 2)))
    shared["consts"] = _CONSTS
    return shared


def make_in_map(inp, b, shared=None):
    if shared is None:
        shared = make_shared_inputs(inp)
    m = dict(shared)
    m["x"] = np.ascontiguousarray(np.asarray(inp["x"][b], dtype=np.float32))
    return m


_PROG_CACHE = {}


def kernel(**inputs):
    if "nc" not in _PROG_CACHE:
        _PROG_CACHE["nc"] = Prog().build()
    nc = _PROG_CACHE["nc"]
    shared = make_shared_inputs(inputs)
    in_maps = [make_in_map(inputs, b, shared) for b in range(N_CORES)]
    res = run_bass_kernel_spmd(nc, in_maps, core_ids=list(range(N_CORES)))
    return np.stack([np.asarray(r["out"], dtype=np.float32) for r in res.results], axis=0)
```

```python
import math
import numpy as np
import concourse.bass as bass
import concourse.mybir as mybir
from concourse.bass_utils import run_bass_kernel_spmd

F32 = mybir.dt.float32
BF16 = mybir.dt.bfloat16
AF = mybir.ActivationFunctionType
ALU = mybir.AluOpType
AX = mybir.AxisListType

D = 2048
S = 2048
DEPTH = 2
NT = S // 128
NC_ = D // 128
D_IN = 20512
EPS = 1e-6
N_CORES = 8

OFF = {}
_acc = 0
for _n, _s in (("a_u", 2048), ("a_v", 2048), ("a_z", 2048), ("b_q", 1024), ("b_k", 1024), ("b_v", 2048),
               ("b_z", 2048), ("b_lr", 32), ("c_q", 2048), ("c_k", 2048), ("c_v", 2048), ("c_z", 2048)):
    OFF[_n] = _acc
    _acc += _s
assert _acc == D_IN


class Res:
    __slots__ = ("name", "w", "r", "excl")

    def __init__(self, name, excl=False):
        self.name = name
        self.excl = excl
        self.w = None
        self.r = {}


class FW:
    LIM = 16000
    NDMA = 24

    def __init__(self, nc):
        self.nc = nc
        self.eng = {"pe": nc.tensor, "act": nc.scalar, "dve": nc.vector, "pool": nc.gpsimd, "sp": nc.sync}
        self.sems = {}
        self.cnt = {}
        self.seen = {e: {} for e in self.eng}
        self.pend = {e: ([], []) for e in self.eng}
        self.dma_rr = 0
        self.dma_rr_pool = 0
        self.dma_last = {}
        self.n_inst = {e: 0 for e in self.eng}

    def sem(self, g, ep):
        k = (g, ep)
        if k not in self.sems:
            self.sems[k] = self.nc.alloc_semaphore(name=f"s_{g}_{ep}")
        return self.sems[k]

    def _next(self, g, inc):
        ep, v = self.cnt.get(g, (0, 0))
        if v + inc > self.LIM:
            ep, v = ep + 1, 0
        v += inc
        self.cnt[g] = (ep, v)
        return (g, ep, v)

    def _wait(self, e, toks):
        best = {}
        for t in toks:
            if t is None:
                continue
            g, ep, v = t
            if g not in best or best[g] < (ep, v):
                best[g] = (ep, v)
        for g, (ep, v) in best.items():
            if self.seen[e].get(g, (-1, 0)) < (ep, v):
                self.eng[e].wait_ge(self.sem(g, ep), v)
                self.n_inst[e] += 1
                self.seen[e][g] = (ep, v)

    def _deps(self, e, reads, writes, same_raw=True):
        toks = []
        for r in reads:
            if r.w is not None and (r.w[0] != e or (same_raw and e != "pe")):
                toks.append(r.w)
        strict = (e == "pool")
        for w in writes:
            if w.w is not None and (w.w[0] != e or e != "pe"):
                toks.append(w.w)
            for g, t in w.r.items():
                if g != e or strict:
                    toks.append(t)
        return toks

    def _update(self, tok, reads, writes):
        g = tok[0]
        for r in reads:
            r.r[g] = tok
        for w in writes:
            w.w = tok
            w.r = {}

    def op(self, e, fn, reads=(), writes=(), inc=True):
        if any(r.excl for r in reads):
            writes = list(writes) + [r for r in reads if r.excl]
            reads = [r for r in reads if not r.excl]
        self._wait(e, self._deps(e, reads, writes))
        ins = fn(self.eng[e])
        self.n_inst[e] += 1
        pr, pw = self.pend[e]
        if not inc:
            pr.extend(reads)
            pw.extend(writes)
            return ins
        tok = self._next(e, 1)
        ins.then_inc(self.sem(tok[0], tok[1]), 1)
        self._update(tok, list(reads) + pr, list(writes) + pw)
        self.pend[e] = ([], [])
        return ins

    def dma(self, e, out, in_, reads=(), writes=(), slow=False):
        if e == "pool":
            g = f"q{self.dma_rr_pool % 6}"
            self.dma_rr_pool += 1
        else:
            g = f"d{self.dma_rr % self.NDMA}"
            self.dma_rr += 1
        toks = self._deps(e, reads, writes)
        toks.append(self.dma_last.get(g))
        self._wait(e, toks)
        tok = self._next(g, 16)
        kw = dict(allow_slow_non_contiguous=True) if slow else {}
        self.eng[e].dma_start(out=out, in_=in_, **kw).then_inc(self.sem(tok[0], tok[1]), 16)
        self.n_inst[e] += 1
        self.dma_last[g] = tok
        self._update(tok, reads, writes)
        return tok

    def barrier(self):
        toks = []
        for g, (ep, v) in self.cnt.items():
            if v > 0:
                toks.append((g, ep, v))
        for e in self.eng:
            assert not self.pend[e][0] and not self.pend[e][1], f"pending non-inc ops on {e}"
            self._wait(e, toks)


class Prog:
    def __init__(self, depth=DEPTH, debug=(), stop_after=None):
        self.depth = depth
        self.debug = set(debug)
        self.stop_after = stop_after
        self.skip = None
        nc = bass.Bass("TRN2", target_bir_lowering=False)
        self.nc = nc
        self.fw = FW(nc)
        L = DEPTH
        dt = lambda name, shape, dtype=F32: nc.dram_tensor(name, list(shape), dtype, kind="ExternalInput").ap()
        self.x = dt("x", [S, D])
        self.norm_pre = dt("norm_pre", [L, D])
        self.w_in = dt("w_in", [L, D, D_IN])
        self.gmlp_ln_g = dt("gmlp_ln_g", [L, D])
        self.gmlp_ln_b = dt("gmlp_ln_b", [L, D])
        self.gmlp_ws = dt("gmlp_ws", [L, 8, 128, 128])
        self.gmlp_bs = dt("gmlp_bs", [L, 8, 128])
        self.gla_wa2 = dt("gla_wa2", [L, 2, 16, 1024])
        self.gla_ba = dt("gla_ba", [L, 2, 1024])
        self.gla_norm = dt("gla_norm", [L, 512])
        self.diff_lambda = dt("diff_lambda", [L, 4, 128])
        self.diff_norm = dt("diff_norm", [L, 256])
        self.bias_tiles = dt("bias_tiles", [8, 3, 128, 128])
        self.bias_far = dt("bias_far", [8, 2])
        self.bias_pat = dt("bias_pat", [8, 6, 128, 512])
        self.w_branch = dt("w_branch", [L, 3, D, D])
        self.w_merge = dt("w_merge", [L, D, 3 * D])
        self.b_merge = dt("b_merge", [L, 3 * D])
        self.w_out = dt("w_out", [L, D, D])
        self.norm_post = dt("norm_post", [L, D])
        self.consts = dt("consts", [8, 128, 128])
        self.out = nc.dram_tensor("out", [S, D], F32, kind="ExternalOutput").ap()

        def scratch(name, shape, dtype=BF16):
            kind = "ExternalOutput" if name in self.debug else "Internal"
            return nc.dram_tensor(name, list(shape), dtype, kind=kind).ap()

        self.scratch = scratch
        self.UT = scratch("UT", [D, S])
        self.ZAT = scratch("ZAT", [D, S])
        self.GV = scratch("GV", [S, D])
        self.QBT = scratch("QBT", [1024, S])
        self.KBT = scratch("KBT", [1024, S])
        self.KB = scratch("KB", [S, 1024])
        self.VB = scratch("VB", [S, D])
        self.ZBT = scratch("ZBT", [D, S])
        self.LRT = scratch("LRT", [32, S], F32)
        self.QCT = scratch("QCT", [D, S])
        self.KCT = scratch("KCT", [D, S])
        self.VC = scratch("VC", [S, D])
        self.ZCT = scratch("ZCT", [D, S])
        self.GT = scratch("GT", [3 * D, S])
        self.BRT = scratch("BRT", [3, D, S])
        self.X1 = scratch("X1", [S, D], F32)
        self.OF = scratch("OF", [S, D], F32)
        self.WB = scratch("WB", [3, D, D])

        self.ARENA_F32 = 51200
        self.arena = nc.alloc_sbuf_tensor("arena", [128, self.ARENA_F32], F32)
        self.psum = [nc.alloc_psum_tensor(f"ps{i}", [128, 512], F32) for i in range(8)]
        self.ps_res = [Res(f"ps{i}", excl=True) for i in range(8)]
        self.ps_rr = 0

    def carve_reset(self):
        self._carve = 0

    def carve(self, nbytes, dtype, shape=None):
        nbytes = (nbytes + 63) // 64 * 64
        a = self._carve // 4
        self._carve += nbytes
        assert self._carve <= self.ARENA_F32 * 4, f"arena overflow {self._carve}"
        ap = self.arena[:, a:a + nbytes // 4]
        if dtype != F32:
            ap = ap.bitcast(dtype)
        return ap

    def bank(self):
        i = self.ps_rr % 8
        self.ps_rr += 1
        return self.psum[i][:, :], self.ps_res[i]

    def phase_norm(self, l, x_src, hT, hT_res):
        fw = self.fw
        ident = self.carve(256, BF16)
        identf = self.carve(512, F32)
        gb = self.carve(8192, F32)
        xt = [self.carve(8192, F32) for _ in range(2)]
        hb = [self.carve(4096, BF16) for _ in range(2)]
        junk = self.carve(4096, BF16)
        st = [self.carve(64, F32) for _ in range(2)]
        r_ident, r_gb, r_junk = Res("ident"), Res("gb"), Res("junk")
        r_xt = [Res("xt0"), Res("xt1")]
        r_hb = [Res("hb0"), Res("hb1")]
        r_st = [Res("st0"), Res("st1")]
        fw.dma("sp", out=identf, in_=self.consts[0], writes=[r_ident])
        fw.op("dve", lambda e: e.tensor_copy(out=ident, in_=identf), reads=[r_ident], writes=[r_ident])
        fw.dma("sp", out=gb, in_=self.norm_pre[l].partition_broadcast(128), writes=[r_gb])
        for t in range(NT):
            s = t % 2
            fw.dma("sp", out=xt[s], in_=x_src[t * 128:(t + 1) * 128, :], writes=[r_xt[s]])
            fw.op("act", lambda e: e.activation(out=junk, in_=xt[s], func=AF.Square, accum_out=st[s][:, 0:1]),
                  reads=[r_xt[s]], writes=[r_junk, r_st[s]])
            fw.op("act", lambda e: e.activation(out=st[s][:, 1:2], in_=st[s][:, 0:1], func=AF.Sqrt, scale=1.0 / D, bias=self.eps_tile),
                  reads=[r_st[s], self.r_const], writes=[r_st[s]])
            fw.op("dve", lambda e: e.reciprocal(out=st[s][:, 2:3], in_=st[s][:, 1:2]), reads=[r_st[s]], writes=[r_st[s]])
            fw.op("dve", lambda e: e.scalar_tensor_tensor(out=hb[s], in0=xt[s], scalar=st[s][:, 2:3], in1=gb,
                                                          op0=ALU.mult, op1=ALU.mult),
                  reads=[r_xt[s], r_st[s], r_gb], writes=[r_hb[s]])
            for half in range(2):
                pt, r_pt = self.bank()
                ptb = pt.bitcast(BF16)
                for c8 in range(8):
                    c = half * 8 + c8
                    fw.op("pe", lambda e: e.transpose(ptb[:, c8 * 128:(c8 + 1) * 128], hb[s][:, c * 128:(c + 1) * 128], ident),
                          reads=[r_hb[s], r_ident], writes=[r_pt], inc=(c8 == 7))
                eng = "act" if half == 0 else "dve"
                src = ptb.rearrange("p (c t) -> p c t", c=8)
                dst = hT[:, half * 8:(half + 1) * 8, t * 128:(t + 1) * 128]
                if eng == "act":
                    fw.op("act", lambda e: e.activation(out=dst, in_=src, func=AF.Copy), reads=[r_pt], writes=[hT_res])
                else:
                    fw.op("dve", lambda e: e.tensor_copy(out=dst, in_=src), reads=[r_pt], writes=[hT_res])

    def proj_blocks(self, actT, act_res, blocks, nwbuf=2, side_dmas=()):
        fw = self.fw
        wbuf = [self.carve(16 * 512 * 2, BF16).rearrange("p (c n) -> p c n", c=16) for _ in range(nwbuf)]
        r_w = [Res(f"w{i}") for i in range(nwbuf)]
        NSTG = 3
        stg = [self.carve(8192, F32) for _ in range(NSTG)]
        r_stg = [Res(f"stg{i}") for i in range(NSTG)]
        si = 0
        side_dmas = list(side_dmas)
        every = max(1, len(blocks) // (len(side_dmas) + 1)) if side_dmas else 0
        for bi, b in enumerate(blocks):
            if side_dmas and bi % every == every - 1:
                o_, i_ = side_dmas.pop(0)
                fw.dma("pool", out=o_, in_=i_)
            ws = bi % nwbuf
            ncols = b["w"].shape[1]
            odt = b.get("dtype", BF16)
            func = b.get("func", AF.Copy)
            scale = b.get("scale", 1.0)
            fw.dma("pool", out=wbuf[ws][:, :, :ncols], in_=b["w"].rearrange("(c p) n -> p c n", p=128), writes=[r_w[ws]])
            if b["orient"] == "F":
                for j in range((ncols + 127) // 128):
                    m = min(128, ncols - j * 128)
                    s_ = si % NSTG
                    si += 1
                    so = stg[s_] if odt == F32 else stg[s_].bitcast(BF16)
                    for tg in range(4):
                        ps, r_ps = self.bank()
                        for c in range(16):
                            fw.op("pe", lambda e: e.matmul(ps[:m, :], wbuf[ws][:, c, j * 128:j * 128 + m], actT[:, c, tg * 512:(tg + 1) * 512],
                                                           start=(c == 0), stop=(c == 15)),
                                  reads=[r_w[ws], act_res], writes=[r_ps], inc=(c == 15))
                        bias = b["bias"][:m, j:j + 1] if b.get("bias") is not None else 0.0
                        rd = [r_ps] + ([self.r_const] if b.get("bias") is not None else [])
                        fw.op("act", lambda e: e.activation(out=so[:m, tg * 512:(tg + 1) * 512], in_=ps[:m, :], func=func, scale=scale, bias=bias),
                              reads=rd, writes=[r_stg[s_]])
                    fw.dma("sp", out=b["dst"][j * 128:j * 128 + m, :], in_=so[:m, 0:2048], reads=[r_stg[s_]])
            else:
                assert ncols == 512
                for t4 in range(4):
                    s_ = si % NSTG
                    si += 1
                    so = stg[s_] if odt == F32 else stg[s_].bitcast(BF16)
                    for tt in range(4):
                        t = t4 * 4 + tt
                        ps, r_ps = self.bank()
                        for c in range(16):
                            fw.op("pe", lambda e: e.matmul(ps[:, :], actT[:, c, t * 128:(t + 1) * 128], wbuf[ws][:, c, :],
                                                           start=(c == 0), stop=(c == 15)),
                                  reads=[r_w[ws], act_res], writes=[r_ps], inc=(c == 15))
                        fw.op("act", lambda e: e.activation(out=so[:, tt * 512:(tt + 1) * 512], in_=ps[:, :], func=func, scale=scale),
                              reads=[r_ps], writes=[r_stg[s_]])
                    dst = b["dst"][t4 * 512:(t4 + 1) * 512, :].rearrange("(tt p) n -> p tt n", p=128)
                    fw.dma("sp", out=dst, in_=so[:, 0:2048].rearrange("p (tt n) -> p tt n", tt=4), reads=[r_stg[s_]])

        for o_, i_ in side_dmas:
            fw.dma("pool", out=o_, in_=i_)

    def inproj_blocks(self, l, bm_tile):
        W = self.w_in[l]
        blocks = []

        def add(name, width, orient, dst, func=AF.Copy, scale=1.0, dtype=BF16):
            o = OFF[name]
            step = 512
            for c0 in range(0, width, step):
                n = min(step, width - c0)
                d = dst[c0:c0 + n, :] if orient == "F" else dst[:, c0:c0 + n]
                blocks.append(dict(w=W[:, o + c0:o + c0 + n], orient=orient, func=func, scale=scale, dst=d, dtype=dtype))

        add("a_u", 2048, "F", self.UT, AF.Gelu_apprx_tanh)
        add("a_v", 2048, "T", self.GV, AF.Gelu_apprx_tanh)
        add("a_z", 2048, "F", self.ZAT, AF.Silu)
        add("b_q", 1024, "F", self.QBT, AF.Copy, 1.0 / 16.0)
        add("b_k", 1024, "F", self.KBT)
        add("b_v", 2048, "T", self.VB)
        add("b_z", 2048, "F", self.ZBT, AF.Silu)
        add("b_lr", 32, "F", self.LRT, AF.Copy, 1.0, F32)
        add("c_q", 2048, "F", self.QCT, AF.Copy, 128.0 ** -0.5)
        add("c_k", 2048, "F", self.KCT)
        add("c_v", 2048, "T", self.VC)
        add("c_z", 2048, "F", self.ZCT, AF.Silu)
        Wm = self.w_merge[l]
        for c0 in range(0, 3 * D, 512):
            blocks.append(dict(w=Wm[:, c0:c0 + 512], orient="F", func=AF.Sigmoid, dst=self.GT[c0:c0 + 512, :],
                               bias=bm_tile[:, c0 // 128:c0 // 128 + 4]))
        return blocks

    def load_ident(self):
        fw = self.fw
        cf = self.carve(8 * 128 * 4, F32).rearrange("p (k n) -> p k n", k=8)
        cb = self.carve(8 * 128 * 2, BF16).rearrange("p (k n) -> p k n", k=8)
        r = Res("cmat")
        fw.dma("sp", out=cf, in_=self.consts.rearrange("k p n -> p k n"), writes=[r])
        fw.op("dve", lambda e: e.tensor_copy(out=cb, in_=cf), reads=[r], writes=[r])
        return cf, cb, r

    def rstd_from(self, ss_ap, out_ap, tmp_ap, n, res):
        fw = self.fw
        fw.op("act", lambda e: e.activation(out=tmp_ap, in_=ss_ap, func=AF.Ln, scale=1.0 / n, bias=self.eps_tile),
              reads=[res, self.r_const], writes=[res])
        fw.op("act", lambda e: e.activation(out=out_ap, in_=tmp_ap, func=AF.Exp, scale=-0.5), reads=[res], writes=[res])

    def phase_mixer_a(self, l):
        fw = self.fw
        cf, cb, r_c = self.load_ident()
        ident = cb[:, 0, :]
        lng = self.carve(8192, F32)
        lnb = self.carve(8192, F32)
        bsf = self.carve(16 * 128 * 4, F32).rearrange("p (c n) -> p c n", c=16)
        wsn = self.carve(8 * 128 * 4, F32).rearrange("p (g q) -> p g q", g=8)
        wsb = self.carve(8 * 128 * 2, BF16).rearrange("p (g q) -> p g q", g=8)
        wsT = self.carve(8 * 128 * 2, BF16).rearrange("p (g q) -> p g q", g=8)
        r_p = Res("a_params")
        fw.dma("sp", out=lng, in_=self.gmlp_ln_g[l].partition_broadcast(128), writes=[r_p])
        fw.dma("sp", out=lnb, in_=self.gmlp_ln_b[l].partition_broadcast(128), writes=[r_p])
        bsv = bsf.rearrange("p (g two) n -> p g two n", two=2)
        for two in range(2):
            fw.dma("sp", out=bsv[:, :, two, :], in_=self.gmlp_bs[l].partition_broadcast(128), writes=[r_p])
        fw.dma("sp", out=wsn, in_=self.gmlp_ws[l].rearrange("g p q -> p g q"), writes=[r_p])
        fw.op("dve", lambda e: e.tensor_copy(out=wsb, in_=wsn), reads=[r_p], writes=[r_p])
        for half in range(2):
            pt, r_pt = self.bank()
            ptb = pt.bitcast(BF16)
            for g4 in range(4):
                g = half * 4 + g4
                fw.op("pe", lambda e: e.transpose(ptb[:, g4 * 128:(g4 + 1) * 128], wsb[:, g, :], ident),
                      reads=[r_p, r_c], writes=[r_pt], inc=(g4 == 3))
            fw.op("dve", lambda e: e.tensor_copy(out=wsT[:, half * 4:(half + 1) * 4, :], in_=ptb[:, 0:512].rearrange("p (g n) -> p g n", g=4)),
                  reads=[r_pt], writes=[r_p])
        NB = 2
        ut = [self.carve(16 * 512 * 2, BF16).rearrange("p (c t) -> p c t", c=16) for _ in range(NB)]
        za = [self.carve(16 * 512 * 2, BF16).rearrange("p (c t) -> p c t", c=16) for _ in range(NB)]
        br = [self.carve(16 * 512 * 2, BF16).rearrange("p (c t) -> p c t", c=16) for _ in range(NB)]
        r_ut = [Res("ut0"), Res("ut1")]
        r_br = [Res("br0"), Res("br1")]
        gv = [self.carve(4096, BF16) for _ in range(2)]
        r_gv = [Res("gv0"), Res("gv1")]
        tmpf = [self.carve(8192, F32) for _ in range(2)]
        r_tmp = [Res("tmpf0"), Res("tmpf1")]
        svn = [self.carve(4096, BF16) for _ in range(2)]
        r_svn = [Res("svn0"), Res("svn1")]
        junk = self.carve(4096, BF16)
        r_junk = Res("junk")
        st = [self.carve(64, F32) for _ in range(2)]
        r_st = [Res("st0"), Res("st1")]
        t1 = [self.carve(2048, F32) for _ in range(2)]
        r_t1 = [Res("t1a"), Res("t1b")]
        cnt = dict(k1=0)

        def load_tg(tg):
            b_ = tg % NB
            fw.dma("sp", out=ut[b_], in_=self.UT[:, tg * 512:(tg + 1) * 512].rearrange("(c p) t -> p c t", p=128), writes=[r_ut[b_]])
            fw.dma("sp", out=za[b_], in_=self.ZAT[:, tg * 512:(tg + 1) * 512].rearrange("(c p) t -> p c t", p=128), writes=[r_ut[b_]])
            fw.op("pool", lambda e: e.tensor_tensor(out=ut[b_], in0=ut[b_], in1=za[b_], op=ALU.mult), reads=[r_ut[b_]], writes=[r_ut[b_]])

        def stage_ln(n):
            s = n % 2
            fw.dma("sp", out=gv[s], in_=self.GV[n * 128:(n + 1) * 128, :], writes=[r_gv[s]])
            fw.op("act", lambda e: e.activation(out=junk, in_=gv[s], func=AF.Square, accum_out=st[s][:, 0:1]),
                  reads=[r_gv[s]], writes=[r_junk, r_st[s]])
            fw.op("dve", lambda e: e.reduce_sum(out=st[s][:, 1:2], in_=gv[s], axis=AX.X), reads=[r_gv[s]], writes=[r_st[s]])
            fw.op("dve", lambda e: e.tensor_scalar(out=st[s][:, 2:3], in0=st[s][:, 1:2], scalar1=1.0 / D, scalar2=None, op0=ALU.mult),
                  reads=[r_st[s]], writes=[r_st[s]])
            fw.op("dve", lambda e: e.tensor_tensor(out=st[s][:, 3:4], in0=st[s][:, 2:3], in1=st[s][:, 2:3], op=ALU.mult),
                  reads=[r_st[s]], writes=[r_st[s]])
            fw.op("dve", lambda e: e.scalar_tensor_tensor(out=st[s][:, 4:5], in0=st[s][:, 0:1], scalar=1.0 / D, in1=st[s][:, 3:4],
                                                          op0=ALU.mult, op1=ALU.subtract),
                  reads=[r_st[s]], writes=[r_st[s]])
            self.rstd_from(st[s][:, 4:5], st[s][:, 6:7], st[s][:, 5:6], 1, r_st[s])
            fw.op("dve", lambda e: e.tensor_scalar(out=tmpf[s], in0=gv[s], scalar1=st[s][:, 2:3], scalar2=st[s][:, 6:7],
                                                   op0=ALU.subtract, op1=ALU.mult),
                  reads=[r_gv[s], r_st[s]], writes=[r_tmp[s]])
            fw.op("pool", lambda e: e.tensor_tensor(out=tmpf[s], in0=tmpf[s], in1=lng, op=ALU.mult), reads=[r_tmp[s], r_p], writes=[r_tmp[s]])
            fw.op("dve", lambda e: e.tensor_tensor(out=svn[s], in0=tmpf[s], in1=lnb, op=ALU.add), reads=[r_tmp[s], r_p], writes=[r_svn[s]])

        def stage_mm(n):
            s = n % 2
            tg, nn = n // 4, n % 4
            b_ = tg % NB
            for q4 in range(4):
                ps, r_ps = self.bank()
                for c4 in range(4):
                    cc = q4 * 4 + c4
                    fw.op("pe", lambda e: e.matmul(ps[:, c4 * 128:(c4 + 1) * 128], svn[s][:, cc * 128:(cc + 1) * 128], wsT[:, cc // 2, :],
                                                   start=True, stop=True),
                          reads=[r_svn[s], r_p], writes=[r_ps], inc=(c4 == 3))
                k = cnt["k1"] % 2
                cnt["k1"] += 1
                t1v = t1[k].rearrange("p (c n) -> p c n", c=4)
                fw.op("dve", lambda e: e.tensor_tensor(out=t1v, in0=ps.rearrange("p (c n) -> p c n", c=4), in1=bsf[:, q4 * 4:(q4 + 1) * 4, :], op=ALU.add),
                      reads=[r_ps, r_p], writes=[r_t1[k]])
                fw.op("pool", lambda e: e.tensor_tensor(out=br[b_][:, q4 * 4:(q4 + 1) * 4, nn * 128:(nn + 1) * 128], in0=t1v,
                                                        in1=ut[b_][:, q4 * 4:(q4 + 1) * 4, nn * 128:(nn + 1) * 128], op=ALU.mult),
                      reads=[r_t1[k], r_ut[b_]], writes=[r_br[b_]])
            if nn == 3:
                fw.dma("sp", out=self.BRT[0][:, tg * 512:(tg + 1) * 512].rearrange("(c p) t -> p c t", p=128), in_=br[b_], reads=[r_br[b_]])

        load_tg(0)
        stage_ln(0)
        for n in range(16):
            if n + 1 < 16:
                if (n + 1) % 4 == 0:
                    load_tg((n + 1) // 4)
                stage_ln(n + 1)
            stage_mm(n)

    def phase_mixer_c(self, l):
        fw = self.fw
        lam_init = 0.8 - 0.6 * math.exp(-0.3 * l)
        cf, cb, r_c = self.load_ident()
        ident = cb[:, 0, :]
        r_p = Res("c_params")
        lv = self.carve(512 * 4, F32).rearrange("p (k n) -> p k n", k=4)
        lt = self.carve(256 * 4, F32).rearrange("p (k n) -> p k n", k=2)
        sc = self.carve(64, F32)
        fw.dma("sp", out=lv, in_=self.diff_lambda[l].partition_broadcast(128), writes=[r_p])
        lvv = lv.rearrange("p (a b) n -> p a b n", b=2)
        fw.op("dve", lambda e: e.tensor_tensor(out=lt, in0=lvv[:, :, 0, :], in1=lvv[:, :, 1, :], op=ALU.mult), reads=[r_p], writes=[r_p])
        fw.op("dve", lambda e: e.reduce_sum(out=sc[:, 0:2], in_=lt, axis=AX.X), reads=[r_p], writes=[r_p])
        fw.op("act", lambda e: e.activation(out=sc[:, 2:4], in_=sc[:, 0:2], func=AF.Exp), reads=[r_p], writes=[r_p])
        fw.op("dve", lambda e: e.scalar_tensor_tensor(out=sc[:, 4:5], in0=sc[:, 3:4], scalar=-lam_init, in1=sc[:, 2:3], op0=ALU.add, op1=ALU.subtract),
              reads=[r_p], writes=[r_p])
        nlam = sc[:, 4:5]
        onec = sc[:, 5:6]
        fw.op("dve", lambda e: e.memset(onec, 1.0), writes=[r_p])
        dnb = self.carve(256 * 4, F32)
        fw.dma("sp", out=dnb, in_=self.diff_norm[l].partition_broadcast(128), writes=[r_p])
        fw.op("dve", lambda e: e.tensor_scalar(out=dnb, in0=dnb, scalar1=1.0 - lam_init, scalar2=None, op0=ALU.mult), reads=[r_p], writes=[r_p])
        bfar = self.carve(64, F32)
        fw.dma("sp", out=bfar[:, 0:16], in_=self.bias_far.rearrange("h s -> (h s)").partition_broadcast(128), writes=[r_p])
        qT = [self.carve(2 * 2048 * 2, BF16).rearrange("p (c t) -> p c t", c=2) for _ in range(2)]
        kT = [self.carve(2 * 2048 * 2, BF16).rearrange("p (c t) -> p c t", c=2) for _ in range(2)]
        vv = [self.carve(16 * 272 * 2, BF16).rearrange("p (k e) -> p k e", k=16) for _ in range(2)]
        bt = [self.carve(3 * 128 * 4, F32).rearrange("p (k n) -> p k n", k=3) for _ in range(2)]
        bp = [self.carve(6 * 512 * 4, F32).rearrange("p (k n) -> p k n", k=6) for _ in range(2)]
        r_h = [Res("hd0"), Res("hd1")]
        for i in range(2):
            fw.op("pool", lambda e: e.memset(vv[i][:, :, 256:272], 0.0), writes=[r_h[i]])
            fw.op("pool", lambda e: e.memset(vv[i][:, :, 256:257], 1.0), writes=[r_h[i]])
        zc = [self.carve(2 * 512 * 2, BF16).rearrange("p (c t) -> p c t", c=2) for _ in range(2)]
        r_zc = [Res("zc0"), Res("zc1")]
        brt = [self.carve(2 * 512 * 2, BF16).rearrange("p (c t) -> p c t", c=2) for _ in range(2)]
        r_brt = [Res("brt0"), Res("brt1")]
        NPT = 4
        pT = [self.carve(512 * 2, BF16) for _ in range(NPT)]
        r_pT = [Res(f"pT{i}") for i in range(NPT)]
        tmpb = [self.carve(512 * 4, F32) for _ in range(3)]
        r_tmpb = [Res(f"tmpb{i}") for i in range(3)]
        o1 = self.carve(4 * 256 * 4, F32).rearrange("p (q e) -> p q e", q=4)
        r_o1 = Res("o1")
        oc = [self.carve(256 * 4, F32) for _ in range(8)]
        r_oc = [Res(f"oc{i}") for i in range(8)]
        raw = [self.carve(260 * 4, F32) for _ in range(8)]
        r_raw = [Res(f"raw{i}") for i in range(8)]
        yb = [self.carve(256 * 2, BF16) for _ in range(8)]
        r_yb = [Res(f"yb{i}") for i in range(8)]
        junk = self.carve(256 * 4, F32)
        r_junk = Res("junk")
        st = [self.carve(64, F32) for _ in range(2)]
        r_st = [Res("st0"), Res("st1")]
        st2 = [self.carve(64, F32) for _ in range(2)]
        r_st2 = [Res("st20"), Res("st21")]
        acc_banks = [0, 1, 2, 3]
        s_banks = [4, 5, 6]
        tp_bank = 7
        NPT_ = NPT
        state = dict(tbi=0, yi=0, loaded_h=-1, loaded_zc=-1)
        nheads = getattr(self, 'c_heads', 8)
        steps = [(h, qg, c, kt) for h in range(nheads) for qg in range(4) for c in range(2) for kt in range(16)]

        def load_head(h):
            hb = h % 2
            if state["loaded_h"] < h and h < nheads:
                state["loaded_h"] = h
                fw.dma("sp", out=qT[hb], in_=self.QCT[h * 256:(h + 1) * 256, :].rearrange("(c p) t -> p c t", p=128), writes=[r_h[hb]])
                fw.dma("sp", out=kT[hb], in_=self.KCT[h * 256:(h + 1) * 256, :].rearrange("(c p) t -> p c t", p=128), writes=[r_h[hb]])
                fw.dma("sp", out=vv[hb][:, :, 0:256], in_=self.VC[:, h * 256:(h + 1) * 256].rearrange("(k p) e -> p k e", p=128), writes=[r_h[hb]])
                fw.dma("sp", out=bp[hb], in_=self.bias_pat[h].rearrange("k p n -> p k n"), writes=[r_h[hb]])

        def load_zc(g):
            if state["loaded_zc"] < g and g < nheads * 4:
                state["loaded_zc"] = g
                h, qg = g // 4, g % 4
                zb = g % 2
                fw.dma("sp", out=zc[zb], in_=self.ZCT[h * 256:(h + 1) * 256, qg * 512:(qg + 1) * 512].rearrange("(c p) t -> p c t", p=128),
                       writes=[r_zc[zb]])

        def ensure_loads(h, qg):
            load_head(h)
            load_zc(h * 4 + qg)

        def emit_qk(i):
            h, qg, c, kt = steps[i]
            hb = h % 2
            ensure_loads(h, qg)
            sb_ = s_banks[i % 3]
            ps, r_ps = self.psum[sb_][:, :], self.ps_res[sb_]
            fw.op("pe", lambda e: e.matmul(ps, kT[hb][:, c, kt * 128:(kt + 1) * 128], qT[hb][:, c, qg * 512:(qg + 1) * 512],
                                           start=True, stop=True), reads=[r_h[hb]], writes=[r_ps])

        def emit_exp(i):
            h, qg, c, kt = steps[i]
            hb = h % 2
            sb_ = s_banks[i % 3]
            ps, r_ps = self.psum[sb_][:, :], self.ps_res[sb_]
            pt_ = i % NPT_
            dl = [kt - (qg * 4 + j) for j in range(4)]
            if all(d_ >= 2 for d_ in dl) or all(d_ <= -2 for d_ in dl):
                side = 1 if dl[0] > 0 else 0
                fw.op("act", lambda e: e.activation(out=pT[pt_], in_=ps, func=AF.Exp, bias=bfar[:, h * 2 + side:h * 2 + side + 1]),
                      reads=[r_ps, r_p], writes=[r_pT[pt_]])
            else:
                tb = state["tbi"] % 3
                state["tbi"] += 1
                p_ = (kt - 4 * qg) + 1
                fw.op("dve", lambda e: e.tensor_tensor(out=tmpb[tb], in0=ps, in1=bp[hb][:, p_, :], op=ALU.add),
                      reads=[r_ps, r_h[hb]], writes=[r_tmpb[tb]])
                fw.op("act", lambda e: e.activation(out=pT[pt_], in_=tmpb[tb], func=AF.Exp), reads=[r_tmpb[tb]], writes=[r_pT[pt_]])

        def emit_pv(i):
            h, qg, c, kt = steps[i]
            hb = h % 2
            pt_ = i % NPT_
            for j in range(4):
                ab = acc_banks[j]
                fw.op("pe", lambda e: e.matmul(self.psum[ab][:, 0:264], pT[pt_][:, j * 128:(j + 1) * 128], vv[hb][:, kt, 0:264],
                                               start=(kt == 0), stop=(kt == 15)),
                      reads=[r_pT[pt_], r_h[hb]], writes=[self.ps_res[ab]], inc=(j == 3))

        deferred = []

        def emit_epilogue(i):
            h, qg, c, kt = steps[i]
            zb = (h * 4 + qg) % 2
            g_ = state["yi"] % 2
            state["yi"] += 1
            stg_, r_stg_ = st[g_], r_st[g_]
            raws = [raw[g_ * 4 + j] for j in range(4)]
            r_raws = [r_raw[g_ * 4 + j] for j in range(4)]
            for j in range(4):
                pa, r_pa = self.psum[acc_banks[j]], self.ps_res[acc_banks[j]]
                fw.op("dve", lambda e: e.tensor_copy(out=raws[j][:, 0:257], in_=pa[:, 0:257]), reads=[r_pa], writes=[r_raws[j]])
            ocs = [oc[g_ * 4 + j] for j in range(4)]
            r_ocs = [r_oc[g_ * 4 + j] for j in range(4)]

            def p_recip():
                for j in range(4):
                    fw.op("dve", lambda e: e.reciprocal(out=stg_[:, j:j + 1], in_=raws[j][:, 256:257]), reads=[r_raws[j]], writes=[r_stg_])

            def p_o1():
                for j in range(4):
                    fw.op("pool", lambda e: e.tensor_scalar(out=o1[:, j, :], in0=raws[j][:, 0:256], scalar1=stg_[:, j:j + 1], scalar2=onec, op0=ALU.mult, op1=ALU.mult),
                          reads=[r_raws[j], r_stg_], writes=[r_o1])

            def p_oc():
                for j in range(4):
                    fw.op("pool", lambda e: e.tensor_scalar(out=ocs[j], in0=raws[j][:, 0:256], scalar1=stg_[:, j:j + 1], scalar2=nlam, op0=ALU.mult, op1=ALU.mult),
                          reads=[r_raws[j], r_stg_, r_p], writes=[r_ocs[j]])
                    fw.op("pool", lambda e: e.tensor_tensor(out=ocs[j], in0=ocs[j], in1=o1[:, j, :], op=ALU.add), reads=[r_ocs[j], r_o1], writes=[r_ocs[j]])

            def p_ss():
                fw.op("pool", lambda e: e.memset(stg2_[:, 0:4], 0.0), writes=[r_stg2_])
                for j in range(4):
                    fw.op("act", lambda e: e.activation(out=junk, in_=ocs[j], func=AF.Square, accum_out=stg2_[:, j:j + 1]),
                          reads=[r_ocs[j]], writes=[r_junk, r_stg2_])

            def p_rstd():
                self.rstd_from(stg2_[:, 0:4], stg2_[:, 4:8], stg2_[:, 8:12], 256, r_stg2_)

            def p_yb():
                for j in range(4):
                    fw.op("pool", lambda e: e.tensor_scalar(out=ocs[j], in0=ocs[j], scalar1=stg2_[:, 4 + j:5 + j], scalar2=onec, op0=ALU.mult, op1=ALU.mult),
                          reads=[r_ocs[j], r_stg2_], writes=[r_ocs[j]])
                    fw.op("pool", lambda e: e.tensor_tensor(out=yb[g_ * 4 + j], in0=ocs[j], in1=dnb, op=ALU.mult), reads=[r_ocs[j], r_p], writes=[r_yb[g_ * 4 + j]])

            def p_tp():
                ptp, r_tp = self.psum[tp_bank][:, :].bitcast(BF16), self.ps_res[tp_bank]
                for j in range(4):
                    for ec in range(2):
                        fw.op("pe", lambda e: e.transpose(ptp[:, (j * 2 + ec) * 128:(j * 2 + ec + 1) * 128], yb[g_ * 4 + j][:, ec * 128:(ec + 1) * 128], ident),
                              reads=[r_yb[g_ * 4 + j], r_c], writes=[r_tp], inc=(j == 3 and ec == 1))

            def p_brt():
                ptp, r_tp = self.psum[tp_bank][:, :].bitcast(BF16), self.ps_res[tp_bank]
                for j in range(4):
                    fw.op("dve", lambda e: e.tensor_tensor(out=brt[zb][:, :, j * 128:(j + 1) * 128],
                                                           in0=ptp[:, j * 256:(j + 1) * 256].rearrange("p (c t) -> p c t", c=2),
                                                           in1=zc[zb][:, :, j * 128:(j + 1) * 128], op=ALU.mult),
                          reads=[r_tp, r_zc[zb]], writes=[r_brt[zb]])
                fw.dma("sp", out=self.BRT[2][h * 256:(h + 1) * 256, qg * 512:(qg + 1) * 512].rearrange("(c p) t -> p c t", p=128), in_=brt[zb],
                       reads=[r_brt[zb]])

            stg2_, r_stg2_ = st2[g_], r_st2[g_]
            defer(i + 1, p_recip)
            if c == 0:
                defer(i + 2, p_o1)
                return
            defer(i + 2, p_oc)
            defer(i + 8, p_ss)
            defer(i + 11, p_rstd)
            defer(i + 12, p_yb)
            defer(i + 19, p_tp)
            defer(i + 21, p_brt)

        def defer(at, fn):
            deferred.append((at, fn))
            deferred.sort(key=lambda t: t[0])

        if getattr(self, 'c_stop', 0) == 1:
            return
        SKEW = 2
        n = len(steps)
        for i in range(min(SKEW, n)):
            emit_qk(i)
        for i in range(n):
            emit_exp(i)
            if i + SKEW < n:
                emit_qk(i + SKEW)
            emit_pv(i)
            h_, qg_, c_, kt_ = steps[i]
            if qg_ == 0 and c_ == 0 and kt_ == 2:
                load_head(h_ + 1)
            if c_ == 1 and kt_ == 8:
                load_zc(h_ * 4 + qg_ + 1)
            if steps[i][3] == 15:
                emit_epilogue(i)
            while deferred and deferred[0][0] <= i:
                deferred.pop(0)[1]()
        while deferred:
            deferred.pop(0)[1]()


    def phase_mixer_b(self, l):
        fw = self.fw
        cf, cb, r_c = self.load_ident()
        ident = cb[:, 0, :]
        r_p = Res("b_params")
        waf = self.carve(2 * 1024 * 4, F32).rearrange("p (k n) -> p k n", k=2)
        wab = self.carve(2 * 1024 * 2, BF16).rearrange("p (k n) -> p k n", k=2)
        lrf = self.carve(2048 * 4, F32)
        lrb = self.carve(2048 * 2, BF16)
        gnb = self.carve(512 * 4, F32)
        fw.op("dve", lambda e: e.memset(waf[0:64], 0.0), writes=[r_p])
        fw.op("dve", lambda e: e.memset(lrf[32:64], 1.0), writes=[r_p])
        fw.dma("sp", out=waf[0:16, 0, :], in_=self.gla_wa2[l, 0], writes=[r_p])
        fw.dma("sp", out=waf[16:32, 1, :], in_=self.gla_wa2[l, 1], writes=[r_p])
        fw.dma("sp", out=waf[32:33, :, :], in_=self.gla_ba[l:l + 1], writes=[r_p])
        fw.dma("sp", out=lrf[0:32, :], in_=self.LRT, writes=[r_p])
        fw.dma("sp", out=gnb, in_=self.gla_norm[l].partition_broadcast(128), writes=[r_p])
        fw.op("dve", lambda e: e.tensor_copy(out=wab[0:64], in_=waf[0:64]), reads=[r_p], writes=[r_p])
        fw.op("dve", lambda e: e.tensor_copy(out=lrb[0:64], in_=lrf[0:64]), reads=[r_p], writes=[r_p])
        Sf = self.carve(4 * 2 * 512 * 4, F32).rearrange("p (h c e) -> p h c e", h=4, c=2)
        Sb = self.carve(4 * 2 * 512 * 2, BF16).rearrange("p (h c e) -> p h c e", h=4, c=2)
        r_S = [[Res(f"S{h}{c}") for c in range(2)] for h in range(4)]
        r_Sb = [[Res(f"Sb{h}{c}") for c in range(2)] for h in range(4)]

        def c3(n, dt_, k):
            return self.carve(n * (4 if dt_ == F32 else 2), dt_).rearrange("p (k n) -> p k n", k=k)

        qT = [c3(1024, BF16, 8) for _ in range(2)]
        kT = [c3(1024, BF16, 8) for _ in range(2)]
        kk = [self.carve(1024 * 2, BF16) for _ in range(2)]
        vt = [self.carve(2048 * 2, BF16) for _ in range(2)]
        r_in = [Res("in0"), Res("in1")]
        r_kk = [Res("kk0"), Res("kk1")]
        e1 = self.carve(1024 * 4, F32)
        r_e1 = Res("e1")
        gneg = self.carve(1024 * 2, BF16)
        r_g = Res("gneg")
        E1 = c3(1024, F32, 8)
        E2 = c3(1024, F32, 8)
        E3 = c3(1024, F32, 8)
        FF = self.carve(1024 * 4, F32)
        r_E1, r_E2, r_E3, r_FF = Res("E1"), Res("E2"), Res("E3"), Res("FF")
        qg = [c3(1024, BF16, 8) for _ in range(2)]
        kg = [c3(1024, BF16, 8) for _ in range(2)]
        qi = [c3(1024, BF16, 8) for _ in range(2)]
        ks = [self.carve(1024 * 2, BF16) for _ in range(2)]
        dec = [self.carve(64, F32).rearrange("p (k n) -> p k n", k=8) for _ in range(2)]
        r_qg = [Res("qg0"), Res("qg1")]; r_kg = [Res("kg0"), Res("kg1")]; r_qi = [Res("qi0"), Res("qi1")]; r_ks = [Res("ks0"), Res("ks1")]; r_dec = [Res("dec0"), Res("dec1")]
        sT = [self.carve(128 * 2, BF16) for _ in range(4)]
        r_sT = [Res(f"sT{i}") for i in range(4)]
        oft = [self.carve(2048 * 4, F32).rearrange("p (h e) -> p h e", h=4) for _ in range(2)]
        r_oft = [Res("oft0"), Res("oft1")]
        osum = [self.carve(512 * 4, F32) for _ in range(4)]
        r_os = [Res(f"os{i}") for i in range(4)]
        pend2, pend3 = [], []
        junk = self.carve(512 * 4, F32)
        r_junk = Res("junk")
        st = [self.carve(64, F32) for _ in range(4)]
        r_st = [Res(f"st{i}") for i in range(4)]
        yb = [self.carve(512 * 2, BF16) for _ in range(4)]
        r_yb = [Res(f"yb{i}") for i in range(4)]
        zball = [c3(2048, BF16, 16) for _ in range(2)]
        r_zball = [Res("zball0"), Res("zball1")]
        brt = [c3(512, BF16, 4) for _ in range(4)]
        r_brt = [Res(f"brt{i}") for i in range(4)]
        P = self.psum
        R = self.ps_res
        r_OF = [Res(f"OF{t}") for t in range(16)]
        A = (0, 1)
        C = 2
        Db = (3, 4)
        Eb = (5, 6)
        TP = 7
        nt = getattr(self, 'b_tiles', 16)
        steps = []
        for dr in range(2):
            tiles = list(range(16)) if dr == 0 else list(range(15, -1, -1))
            for ti, T in enumerate(tiles[:nt]):
                steps.append((dr, ti, T))
        cnt = dict(e=0, s=0)

        def prologue(si):
            dr, ti, T = steps[si]
            b_ = si % 2
            OPc, OPs, OPd = (cb[:, 1, :], cb[:, 3, :], cb[:, 5, :]) if dr == 0 else (cb[:, 2, :], cb[:, 4, :], cb[:, 6, :])
            tsl = slice(T * 128, (T + 1) * 128)
            fw.dma("sp", out=qT[b_], in_=self.QBT[:, tsl].rearrange("(c p) t -> p c t", p=128), writes=[r_in[b_]])
            fw.dma("sp", out=kT[b_], in_=self.KBT[:, tsl].rearrange("(c p) t -> p c t", p=128), writes=[r_in[b_]])
            fw.dma("sp", out=vt[b_], in_=self.VB[tsl, :], writes=[r_in[b_]])
            ktp = P[A[0]][:, :].bitcast(BF16)
            for dk in range(8):
                fw.op("pe", lambda e: e.transpose(ktp[:, dk * 128:(dk + 1) * 128], kT[b_][:, dk, :], ident),
                      reads=[r_in[b_], r_c], writes=[R[A[0]]], inc=(dk == 7))
            fw.op("dve", lambda e: e.tensor_copy(out=kk[b_], in_=ktp), reads=[R[A[0]]], writes=[r_kk[b_]])
            for hf in range(2):
                fw.op("pe", lambda e: e.matmul(P[A[hf]][:, :], lrb[0:64, tsl], wab[0:64, dr, hf * 512:(hf + 1) * 512], start=True, stop=True),
                      reads=[r_p], writes=[R[A[hf]]])
                fw.op("act", lambda e: e.activation(out=e1[:, hf * 512:(hf + 1) * 512], in_=P[A[hf]][:, :], func=AF.Exp, scale=-1.0),
                      reads=[R[A[hf]]], writes=[r_e1])
            fw.op("act", lambda e: e.activation(out=gneg, in_=e1, func=AF.Ln, bias=1.0, scale=1.0), reads=[r_e1], writes=[r_g])
            yield
            for hf in range(2):
                for d4 in range(4):
                    dk = hf * 4 + d4
                    fw.op("pe", lambda e: e.matmul(P[A[hf]][:, d4 * 128:(d4 + 1) * 128], gneg[:, dk * 128:(dk + 1) * 128], OPd, start=True, stop=True),
                          reads=[r_g, r_c], writes=[R[A[hf]]], inc=(d4 == 3))
                src = P[A[hf]][:, :].rearrange("p (k n) -> p k n", k=4)
                fw.op("act", lambda e: e.activation(out=E1[:, hf * 4:(hf + 1) * 4, :], in_=src, func=AF.Exp, scale=-1.0 / 16), reads=[R[A[hf]]], writes=[r_E1])
                fw.op("act", lambda e: e.activation(out=E2[:, hf * 4:(hf + 1) * 4, :], in_=src, func=AF.Exp, scale=1.0 / 16), reads=[R[A[hf]]], writes=[r_E2])
            fw.op("dve", lambda e: e.tensor_tensor(out=qg[b_], in0=qT[b_], in1=E1, op=ALU.mult), reads=[r_in[b_], r_E1], writes=[r_qg[b_]])
            fw.op("dve", lambda e: e.tensor_tensor(out=kg[b_][:, 0:4, :], in0=kT[b_][:, 0:4, :], in1=E2[:, 0:4, :], op=ALU.mult), reads=[r_in[b_], r_E2], writes=[r_kg[b_]])
            fw.op("pool", lambda e: e.tensor_tensor(out=kg[b_][:, 4:8, :], in0=kT[b_][:, 4:8, :], in1=E2[:, 4:8, :], op=ALU.mult), reads=[r_in[b_], r_E2], writes=[r_kg[b_]])
            yield
            lastc = 127 if dr == 0 else 0
            for hf in range(2):
                for d4 in range(4):
                    dk = hf * 4 + d4
                    fw.op("pe", lambda e: e.matmul(P[A[hf]][:, d4 * 128:(d4 + 1) * 128], gneg[:, dk * 128:(dk + 1) * 128], OPc, start=True, stop=True),
                          reads=[r_g, r_c], writes=[R[A[hf]]], inc=(d4 == 3))
                src = P[A[hf]][:, :].rearrange("p (k n) -> p k n", k=4)
                fw.op("act", lambda e: e.activation(out=E3[:, hf * 4:(hf + 1) * 4, :], in_=src, func=AF.Exp, scale=-1.0 / 16), reads=[R[A[hf]]], writes=[r_E3])
                srcd = P[A[hf]][:, :].rearrange("p (k n) -> p k n", k=4)[:, :, lastc:lastc + 1]
                fw.op("act", lambda e: e.activation(out=dec[b_][:, hf * 4:(hf + 1) * 4, 0:1], in_=srcd, func=AF.Exp, scale=-1.0 / 16),
                      reads=[R[A[hf]]], writes=[r_dec[b_]])
            fw.op("dve", lambda e: e.tensor_tensor(out=qi[b_], in0=qT[b_], in1=E3, op=ALU.mult), reads=[r_in[b_], r_E3], writes=[r_qi[b_]])
            yield
            for hf in range(2):
                fw.op("pe", lambda e: e.matmul(P[A[hf]][:, :], OPs, gneg[:, hf * 512:(hf + 1) * 512], start=True, stop=True),
                      reads=[r_g, r_c], writes=[R[A[hf]]])
                fw.op("act", lambda e: e.activation(out=FF[:, hf * 512:(hf + 1) * 512], in_=P[A[hf]][:, :], func=AF.Exp, scale=-1.0 / 16),
                      reads=[R[A[hf]]], writes=[r_FF])
            for hf in range(2):
                fw.op("pool", lambda e: e.tensor_tensor(out=ks[b_][:, hf * 512:(hf + 1) * 512], in0=kk[b_][:, hf * 512:(hf + 1) * 512], in1=FF[:, hf * 512:(hf + 1) * 512], op=ALU.mult),
                      reads=[r_kk[b_], r_FF], writes=[r_ks[b_]])
            yield

        def heads(si):
            dr, ti, T = steps[si]
            b_ = si % 2
            tsl = slice(T * 128, (T + 1) * 128)
            maskf = cf[:, 1, :] if dr == 0 else cf[:, 2, :]
            if ti == 0:
                for h in range(4):
                    for c in range(2):
                        fw.op("dve", lambda e: e.memset(Sf[:, h, c, :], 0.0), writes=[r_S[h][c]])
                        fw.op("pool", lambda e: e.memset(Sb[:, h, c, :], 0.0), writes=[r_Sb[h][c]])
            if dr == 1:
                fw.dma("sp", out=oft[b_], in_=self.OF[tsl, :].rearrange("p (h e) -> p h e", h=4), reads=[r_OF[T]], writes=[r_oft[b_]])
                fw.dma("sp", out=zball[b_], in_=self.ZBT[:, tsl].rearrange("(c p) t -> p c t", p=128), writes=[r_zball[b_]])
            for hp in range(2):
                hs = (2 * hp, 2 * hp + 1)
                for k2, h in enumerate(hs):
                    for dc in range(2):
                        fw.op("pe", lambda e: e.matmul(P[C][:, k2 * 128:(k2 + 1) * 128], kg[b_][:, h * 2 + dc, :], qg[b_][:, h * 2 + dc, :], start=(dc == 0), stop=(dc == 1)),
                              reads=[r_qg[b_], r_kg[b_]], writes=[R[C]], inc=(dc == 1))
                sidx = []
                for k2, h in enumerate(hs):
                    s_ = cnt["s"] % 4
                    cnt["s"] += 1
                    sidx.append(s_)
                    fw.op("dve", lambda e: e.tensor_tensor(out=sT[s_], in0=P[C][:, k2 * 128:(k2 + 1) * 128], in1=maskf, op=ALU.mult), reads=[R[C], r_c], writes=[r_sT[s_]])
                while pend2:
                    pend2.pop(0)()
                for k2, h in enumerate(hs):
                    D_ = Db[k2]
                    for dc in range(2):
                        fw.op("pe", lambda e: e.matmul(P[D_][:, :], qi[b_][:, h * 2 + dc, :], Sb[:, h, dc, :], start=(dc == 0), stop=False),
                              reads=[r_qi[b_], r_Sb[h][dc]], writes=[R[D_]], inc=False)
                    fw.op("pe", lambda e: e.matmul(P[D_][:, :], sT[sidx[k2]], vt[b_][:, h * 512:(h + 1) * 512], start=False, stop=True),
                          reads=[r_sT[sidx[k2]], r_in[b_]], writes=[R[D_]])
                while pend3:
                    pend3.pop(0)()
                for k2, h in enumerate(hs):
                    for dc in range(2):
                        E_ = Eb[cnt["e"] % 2]
                        cnt["e"] += 1
                        fw.op("pe", lambda e: e.matmul(P[E_][:, :], ks[b_][:, (h * 2 + dc) * 128:(h * 2 + dc + 1) * 128], vt[b_][:, h * 512:(h + 1) * 512],
                                                       start=True, stop=True), reads=[r_ks[b_], r_in[b_]], writes=[R[E_]])
                        fw.op("dve", lambda e: e.scalar_tensor_tensor(out=Sf[:, h, dc, :], in0=Sf[:, h, dc, :], scalar=dec[b_][:, h * 2 + dc, 0:1],
                                                                      in1=P[E_][:, :], op0=ALU.mult, op1=ALU.add),
                              reads=[r_S[h][dc], r_dec[b_], R[E_]], writes=[r_S[h][dc]])
                        fw.op("act", lambda e: e.activation(out=Sb[:, h, dc, :], in_=Sf[:, h, dc, :], func=AF.Copy), reads=[r_S[h][dc]], writes=[r_Sb[h][dc]])
                yield
                for k2, h in enumerate(hs):
                    D_ = Db[k2]
                    if dr == 0:
                        fw.op("act", lambda e: e.activation(out=oft[b_][:, h, :], in_=P[D_][:, :], func=AF.Copy), reads=[R[D_]], writes=[r_oft[b_]])
                    else:
                        s_ = h % 4
                        fw.op("dve", lambda e: e.tensor_tensor(out=osum[s_], in0=P[D_][:, :], in1=oft[b_][:, h, :], op=ALU.add),
                              reads=[R[D_], r_oft[b_]], writes=[r_os[s_]])
                        fw.op("act", lambda e: e.activation(out=junk, in_=osum[s_], func=AF.Square, accum_out=st[s_][:, 0:1]),
                              reads=[r_os[s_]], writes=[r_junk, r_st[s_]])
                        self.rstd_from(st[s_][:, 0:1], st[s_][:, 2:3], st[s_][:, 1:2], 512, r_st[s_])

                        def part2(s_=s_, h=h, tsl=tsl, b_=b_):
                            fw.op("dve", lambda e: e.scalar_tensor_tensor(out=yb[s_], in0=osum[s_], scalar=st[s_][:, 2:3], in1=gnb, op0=ALU.mult, op1=ALU.mult),
                                  reads=[r_os[s_], r_st[s_], r_p], writes=[r_yb[s_]])

                        def part3(s_=s_, h=h, tsl=tsl, b_=b_):
                            ptp = P[TP][:, 0:256].bitcast(BF16)
                            for ec in range(4):
                                fw.op("pe", lambda e: e.transpose(ptp[:, ec * 128:(ec + 1) * 128], yb[s_][:, ec * 128:(ec + 1) * 128], ident),
                                      reads=[r_yb[s_], r_c], writes=[R[TP]], inc=(ec == 3))
                            fw.op("dve", lambda e: e.tensor_tensor(out=brt[s_], in0=ptp.rearrange("p (c t) -> p c t", c=4), in1=zball[b_][:, h * 4:(h + 1) * 4, :], op=ALU.mult),
                                  reads=[R[TP], r_zball[b_]], writes=[r_brt[s_]])
                            fw.dma("sp", out=self.BRT[1][h * 512:(h + 1) * 512, tsl].rearrange("(c p) t -> p c t", p=128), in_=brt[s_], reads=[r_brt[s_]])

                        pend2.append(part2)
                        pend3.append(part3)
                if hp == 0:
                    yield
            if dr == 0:
                fw.dma("sp", out=self.OF[tsl, :].rearrange("p (h e) -> p h e", h=4), in_=oft[b_], reads=[r_oft[b_]], writes=[r_OF[T]])
            yield

        for _ in prologue(0):
            pass
        for si in range(len(steps)):
            pg = prologue(si + 1) if si + 1 < len(steps) else iter(())
            for _ in heads(si):
                next(pg, None)
            for _ in pg:
                pass
        while pend2:
            pend2.pop(0)()
        while pend3:
            pend3.pop(0)()


    def phase_merge(self, l, mT, r_mT):
        fw = self.fw
        brt = [self.carve(16 * 512 * 2, BF16).rearrange("p (c t) -> p c t", c=16) for _ in range(3)]
        r_br = [Res(f"brt{i}") for i in range(3)]
        NW = 4
        wb = [self.carve(16 * 512 * 2, BF16).rearrange("p (c n) -> p c n", c=16) for _ in range(NW)]
        r_w = [Res(f"w{i}") for i in range(NW)]
        gt = [self.carve(3 * 512 * 2, BF16).rearrange("p (i t) -> p i t", i=3) for _ in range(2)]
        r_gt = [Res("gt0"), Res("gt1")]
        tt = [[self.carve(512 * 4, F32) for _ in range(3)] for _ in range(2)]
        r_tt = [[Res(f"tt{k}{i}") for i in range(3)] for k in range(2)]
        GTv = self.GT.rearrange("(i c p) t -> c p i t", i=3, p=128)
        wi = 0
        ei = 0
        for tg in range(4):
            tsl = slice(tg * 512, (tg + 1) * 512)
            for i in range(3):
                fw.dma("sp", out=brt[i], in_=self.BRT[i][:, tsl].rearrange("(c p) t -> p c t", p=128), writes=[r_br[i]])
            for dblk in range(4):
                ws = []
                for i in range(3):
                    w_ = wi % NW
                    wi += 1
                    ws.append(w_)
                    fw.dma("sp", out=wb[w_], in_=self.WB[i][:, dblk * 512:(dblk + 1) * 512].rearrange("(c p) n -> p c n", p=128), writes=[r_w[w_]])
                for j in range(4):
                    dsub = dblk * 4 + j
                    k = ei % 2
                    ei += 1
                    fw.dma("sp", out=gt[k], in_=GTv[dsub][:, :, tsl], writes=[r_gt[k]])
                    pss = []
                    for i in range(3):
                        ps, r_ps = self.bank()
                        pss.append((ps, r_ps))
                        for c in range(16):
                            fw.op("pe", lambda e: e.matmul(ps, wb[ws[i]][:, c, j * 128:(j + 1) * 128], brt[i][:, c, :], start=(c == 0), stop=(c == 15)),
                                  reads=[r_w[ws[i]], r_br[i]], writes=[r_ps], inc=(c == 15))
                    for i in range(3):
                        ps, r_ps = pss[i]
                        fw.op("dve", lambda e: e.tensor_tensor(out=tt[k][i], in0=ps, in1=gt[k][:, i, :], op=ALU.mult),
                              reads=[r_ps, r_gt[k]], writes=[r_tt[k][i]])
                    fw.op("pool", lambda e: e.tensor_tensor(out=tt[k][0], in0=tt[k][0], in1=tt[k][1], op=ALU.add),
                          reads=[r_tt[k][0], r_tt[k][1]], writes=[r_tt[k][0]])
                    fw.op("pool", lambda e: e.tensor_tensor(out=mT[:, dsub, tsl], in0=tt[k][0], in1=tt[k][2], op=ALU.add),
                          reads=[r_tt[k][0], r_tt[k][2]], writes=[r_mT])

    def phase_out(self, l, mT, r_mT, x_src, x_dst):
        fw = self.fw
        wo = self.carve(16 * 2048 * 2, BF16).rearrange("p (c n) -> p c n", c=16)
        r_wo = [Res(f"wo{i}") for i in range(4)]
        for nb in range(4):
            fw.dma("pool", out=wo[:, :, nb * 512:(nb + 1) * 512], in_=self.w_out[l][:, nb * 512:(nb + 1) * 512].rearrange("(c p) n -> p c n", p=128),
                   writes=[r_wo[nb]])
        npb = self.carve(8192, F32)
        r_np = Res("npb")
        fw.dma("sp", out=npb, in_=self.norm_post[l].partition_broadcast(128), writes=[r_np])
        xt = [self.carve(8192, F32) for _ in range(2)]
        r_xt = [Res("xt0"), Res("xt1")]
        ot = [self.carve(8192, F32) for _ in range(2)]
        r_ot = [Res("ot0"), Res("ot1")]
        junk = self.carve(2048, F32)
        r_junk = Res("junk")
        st = [self.carve(64, F32) for _ in range(2)]
        r_st = [Res("st0"), Res("st1")]
        for t in range(NT):
            s = t % 2
            fw.dma("sp", out=xt[s], in_=x_src[t * 128:(t + 1) * 128, :], writes=[r_xt[s]])
            banks = []
            for nb in range(4):
                ps, r_ps = self.bank()
                banks.append((ps, r_ps))
                for c in range(16):
                    fw.op("pe", lambda e: e.matmul(ps, mT[:, c, t * 128:(t + 1) * 128], wo[:, c, nb * 512:(nb + 1) * 512], start=(c == 0), stop=(c == 15)),
                          reads=[r_mT, r_wo[nb]], writes=[r_ps], inc=(c == 15))
                fw.op("act", lambda e: e.activation(out=junk, in_=ps, func=AF.Square, accum_out=st[s][:, nb:nb + 1]),
                      reads=[r_ps], writes=[r_junk, r_st[s]])
            fw.op("dve", lambda e: e.reduce_sum(out=st[s][:, 4:5], in_=st[s][:, 0:4], axis=AX.X), reads=[r_st[s]], writes=[r_st[s]])
            self.rstd_from(st[s][:, 4:5], st[s][:, 6:7], st[s][:, 5:6], D, r_st[s])
            for nb in range(4):
                ps, r_ps = banks[nb]
                sl = slice(nb * 512, (nb + 1) * 512)
                fw.op("dve", lambda e: e.scalar_tensor_tensor(out=ot[s][:, sl], in0=ps, scalar=st[s][:, 6:7], in1=npb[:, sl], op0=ALU.mult, op1=ALU.mult),
                      reads=[r_ps, r_st[s], r_np], writes=[r_ot[s]])
            fw.op("pool", lambda e: e.tensor_tensor(out=ot[s], in0=ot[s], in1=xt[s], op=ALU.add), reads=[r_ot[s], r_xt[s]], writes=[r_ot[s]])
            fw.dma("sp", out=x_dst[t * 128:(t + 1) * 128, :], in_=ot[s], reads=[r_ot[s]])

    def setup_consts(self):
        fw = self.fw
        base = self.ARENA_F32 - 64
        self.eps_tile = self.arena[:, base:base + 1]
        self.r_const = Res("const")
        fw.op("dve", lambda e: e.memset(self.eps_tile, EPS), writes=[self.r_const])
        self.ARENA_LIMIT = base * 4

    def build(self):
        fw = self.fw
        self.carve_reset()
        self.setup_consts()
        fw.barrier()
        for l in range(self.depth):
            x_src = self.x if l == 0 else self.X1
            self.carve_reset()
            hT = self.carve(16 * 2048 * 2, BF16).rearrange("p (c t) -> p c t", c=16)
            hT_res = Res("hT")
            mark = self._carve
            self.phase_norm(l, x_src, hT, hT_res)
            fw.barrier()
            if "HT" in self.debug and l == self.depth - 1:
                HT = self.scratch("HT", [D, S])
                fw.dma("sp", out=HT.rearrange("(c p) t -> p c t", p=128), in_=hT, reads=[hT_res])
            if self.stop_after == ("norm", l):
                break
            self._carve = mark
            bm = self.carve(48 * 4, F32)
            fw.dma("sp", out=bm, in_=self.b_merge[l].rearrange("(k p) -> p k", p=128), writes=[self.r_const], slow=True)
            blocks = self.inproj_blocks(l, bm)
            if "only_blocks" in self.__dict__:
                blocks = [blocks[i] for i in self.only_blocks]
            side = []
            for i in range(3):
                for c0 in range(0, D, 512):
                    side.append((self.WB[i][:, c0:c0 + 512].rearrange("(c p) n -> p c n", p=128),
                                 self.w_branch[l, i][:, c0:c0 + 512].rearrange("(c p) n -> p c n", p=128)))
            self.proj_blocks(hT, hT_res, blocks, side_dmas=side)
            fw.barrier()
            if self.stop_after == ("inproj", l):
                break
            for name, fn in (("a", self.phase_mixer_a), ("b", self.phase_mixer_b), ("c", self.phase_mixer_c)):
                if self.skip and name in self.skip:
                    continue
                self.carve_reset()
                fn(l)
                fw.barrier()
            if self.stop_after == ("mix", l):
                break
            self.carve_reset()
            mT = self.carve(16 * 2048 * 2, BF16).rearrange("p (c t) -> p c t", c=16)
            r_mT = Res("mT")
            mark = self._carve
            self.phase_merge(l, mT, r_mT)
            fw.barrier()
            if "MT" in self.debug and l == self.depth - 1:
                MT = self.scratch("MT", [D, S])
                fw.dma("sp", out=MT.rearrange("(c p) t -> p c t", p=128), in_=mT, reads=[r_mT])
                fw.barrier()
            self._carve = mark
            x_dst = self.out if l == self.depth - 1 else self.X1
            self.phase_out(l, mT, r_mT, x_src, x_dst)
            fw.barrier()
        fw.barrier()
        return self.nc


def _t5_buckets_np(rel):
    nb = 16
    max_exact = 8
    rel = np.asarray(rel, np.int32)
    ret = np.where(rel > 0, nb, 0).astype(np.int32)
    n = np.abs(rel).astype(np.int32)
    nf = np.maximum(n, 1).astype(np.float32)
    large = max_exact + (np.log(nf / np.float32(max_exact)) / np.float32(math.log(128 / max_exact))
                         * np.float32(nb - max_exact)).astype(np.int32)
    large = np.minimum(large, nb - 1)
    return ret + np.where(n < max_exact, n, large)


def _make_consts():
    c = np.zeros((8, 128, 128), np.float32)
    c[0] = np.eye(128, dtype=np.float32)
    i = np.arange(128)
    jj, ii = i[:, None], i[None, :]
    same = (jj >= 0) & (ii >= 0)
    le = (same & (jj <= ii)).astype(np.float32)
    ge = (same & (jj >= ii)).astype(np.float32)
    gt = (same & (jj > ii)).astype(np.float32)
    lt = (same & (jj < ii)).astype(np.float32)
    c[1], c[2], c[3], c[4] = le, ge, gt, lt
    reff = (same & (jj <= 63)).astype(np.float32)
    refb = (same & (jj >= 64)).astype(np.float32)
    c[5] = le - reff
    c[6] = ge - refb
    c[7] = 1.0
    return c


_CONSTS = _make_consts()
_kk = np.arange(128)[:, None]
_qq = np.arange(128)[None, :]
_BUCKET_TILES = np.stack([_t5_buckets_np(_kk - _qq + 128 * dlt) for dlt in (-1, 0, 1)])
_BUCKET_PAT = np.stack([np.concatenate([_t5_buckets_np(_kk - _qq + 128 * (d0 - j)) for j in range(4)], axis=1) for d0 in range(-1, 5)])


def make_shared_inputs(inp):
    f = lambda a: np.ascontiguousarray(np.asarray(a, dtype=np.float32))
    rb = f(inp["rel_bias"])
    shared = {k: f(inp[k]) for k in ("norm_pre", "w_in", "gmlp_ln_g", "gmlp_ln_b", "gmlp_ws", "gmlp_bs", "gla_wa2", "gla_ba",
                                      "gla_norm", "diff_lambda", "diff_norm", "w_branch", "w_merge", "b_merge", "w_out", "norm_post")}
    shared["bias_tiles"] = np.ascontiguousarray(np.transpose(rb[_BUCKET_TILES], (3, 0, 1, 2)))
    shared["bias_far"] = np.ascontiguousarray(np.stack([rb[15], rb[31]], axis=1))
    shared["bias_pat"] = np.ascontiguousarray(np.transpose(rb[_BUCKET_PAT], (3, 0, 1, 2)))
    shared["consts"] = _CONSTS
    return shared


def make_in_map(inp, b, shared=None):
    if shared is None:
        shared = make_shared_inputs(inp)
    m = dict(shared)
    m["x"] = np.ascontiguousarray(np.asarray(inp["x"][b], dtype=np.float32))
    return m


_PROG_CACHE = {}


def kernel(**inputs):
    if "nc" not in _PROG_CACHE:
        _PROG_CACHE["nc"] = Prog().build()
    nc = _PROG_CACHE["nc"]
    shared = make_shared_inputs(inputs)
    in_maps = [make_in_map(inputs, b, shared) for b in range(N_CORES)]
    res = run_bass_kernel_spmd(nc, in_maps, core_ids=list(range(N_CORES)))
    return np.stack([np.asarray(r["out"], dtype=np.float32) for r in res.results], axis=0)
```
